# Optimizing a Trainium2 kernel written in Bass

```python
import jax, jax.numpy as jnp
from jax import lax
import numpy as np

D_MODEL = 2048
BATCH = 8
SEQ = 4096
DEPTH = 1

GRID_W = 64
CTX_LEN = 256
D_FF = 5632
MIX_WIDTH = D_MODEL
MLSTM_HEADS = 4
MLSTM_DV = MIX_WIDTH // 2 // MLSTM_HEADS
MLSTM_DK = MLSTM_DV // 2
MLSTM_WIDTH = MLSTM_HEADS * MLSTM_DV
QK_WIDTH = 2 * MLSTM_HEADS * MLSTM_DK
CONV_K = 5
MLSTM_CHUNK = 128
GMLP_CHUNK = 128
GMLP_GROUPS = 8
GMLP_WIDTH = MIX_WIDTH - MLSTM_WIDTH
GMLP_GD = GMLP_WIDTH // GMLP_GROUPS
CHUNK_ROWS = GMLP_CHUNK // GRID_W
N_MOD = 9
EPS = 1e-6
COL_QK_END = QK_WIDTH
COL_V_END = COL_QK_END + MLSTM_WIDTH
COL_O_END = COL_V_END + MLSTM_WIDTH
COL_GATE_END = COL_O_END + 4 * MLSTM_HEADS
IN_COLS = COL_GATE_END + 2 * GMLP_WIDTH

kernel_name = "hymba_mlstm_gmlp_macaron_dit"


def rms_norm(x, g):
    xf = x.astype(jnp.float32)
    y = xf * lax.rsqrt(jnp.mean(xf * xf, axis=-1, keepdims=True) + EPS)
    return (y * g.astype(jnp.float32)).astype(x.dtype)


def modulate(x, shift, scale):
    return x * (1 + scale) + shift


def adaln(cond, w, b):
    mod = jax.nn.silu(cond) @ w + b
    return jnp.split(mod, N_MOD, axis=-1)


def swiglu(x, w_in, w_out):
    gate, up = jnp.split(x @ w_in, 2, axis=-1)
    return (jax.nn.silu(gate) * up) @ w_out


def centred_conv(x, w, b):
    T = x.shape[1]
    p = w.shape[0] // 2
    xp = jnp.pad(x, ((0, 0), (p, p), (0, 0)))
    y = b
    for j in range(w.shape[0]):
        y = y + xp[:, j:j + T] * w[j]
    return y


def _to_chunks(a, n_chunks):
    B, T, H, d = a.shape
    return a.astype(jnp.float32).reshape(B, n_chunks, T // n_chunks, H, d).transpose(1, 0, 3, 2, 4)


def _gate_chunks(a, n_chunks):
    B, T, H = a.shape
    return a.astype(jnp.float32).reshape(B, n_chunks, T // n_chunks, H).transpose(1, 0, 3, 2)


def _state_update(state, k, v, logi, b):
    C, n, m = state
    b_last = b[..., -1]
    g = b_last[..., None] - b + logi
    m_new = jnp.maximum(b_last + m, jnp.max(g, axis=-1))
    decay = jnp.exp(b_last + m - m_new)
    kw = k * jnp.exp(g - m_new[..., None])[..., None]
    C_new = decay[..., None, None] * C + jnp.einsum('bhld,bhle->bhde', kw, v)
    n_new = decay[..., None] * n + jnp.sum(kw, axis=2)
    return (C_new, n_new, m_new)


def _chunk_output(state, q, k, v, logi, b):
    C, n, m = state
    L = q.shape[2]
    tri = jnp.tril(jnp.ones((L, L), dtype=bool))
    d_mat = jnp.where(tri, b[..., :, None] - b[..., None, :] + logi[..., None, :], -jnp.inf)
    m_inter = b + m[..., None]
    m_tok = jnp.maximum(m_inter, jnp.max(d_mat, axis=-1))
    s = jnp.einsum('bhjd,bhld->bhjl', q, k) * jnp.exp(d_mat - m_tok[..., None])
    a = jnp.exp(m_inter - m_tok)
    num = a[..., None] * jnp.einsum('bhjd,bhde->bhje', q, C) + jnp.einsum('bhjl,bhle->bhje', s, v)
    nq = a * jnp.einsum('bhjd,bhd->bhj', q, n) + jnp.sum(s, axis=-1)
    return num / jnp.maximum(jnp.abs(nq), jnp.exp(-m_tok))[..., None]


def mlstm_scan(q, k, v, logi, logf, state):
    B, T, H, _ = q.shape
    nc = T // MLSTM_CHUNK
    xs = (_to_chunks(q, nc), _to_chunks(k, nc), _to_chunks(v, nc),
          _gate_chunks(logi, nc), _gate_chunks(logf, nc))

    def body(carry, inp):
        qc, kc, vc, ic, fc = inp
        b = jnp.cumsum(fc, axis=-1)
        h = _chunk_output(carry, qc, kc, vc, ic, b)
        return _state_update(carry, kc, vc, ic, b), h

    state, h = lax.scan(body, state, xs)
    h = h.transpose(1, 0, 3, 2, 4).reshape(B, T, H, -1)
    return h.astype(v.dtype), state


def mlstm_final_state(k, v, logi, logf, state):
    nc = k.shape[1] // MLSTM_CHUNK
    xs = (_to_chunks(k, nc), _to_chunks(v, nc), _gate_chunks(logi, nc), _gate_chunks(logf, nc))

    def body(carry, inp):
        kc, vc, ic, fc = inp
        return _state_update(carry, kc, vc, ic, jnp.cumsum(fc, axis=-1)), None

    state, _ = lax.scan(body, state, xs)
    return state


def mlstm_inputs(z, conv_w, conv_b, b_igate, b_fgate):
    B, T, _ = z.shape
    qk = jax.nn.silu(centred_conv(z[..., :COL_QK_END], conv_w, conv_b))
    q = qk[..., :QK_WIDTH // 2].reshape(B, T, MLSTM_HEADS, MLSTM_DK) * (MLSTM_DK ** -0.5)
    k = qk[..., QK_WIDTH // 2:].reshape(B, T, MLSTM_HEADS, MLSTM_DK)
    v = z[..., COL_QK_END:COL_V_END].reshape(B, T, MLSTM_HEADS, MLSTM_DV)
    o = jax.nn.sigmoid(z[..., COL_V_END:COL_O_END])
    gates = z[..., COL_O_END:COL_GATE_END].reshape(B, T, 2, 2, MLSTM_HEADS)
    logi = (gates[..., 0, :] + b_igate).astype(jnp.float32)
    logf = jax.nn.log_sigmoid((gates[..., 1, :] + b_fgate).astype(jnp.float32))
    return q, k, v, o, logi, logf


def mlstm_group(zc, zx, conv_w, conv_b, b_igate, b_fgate, norm_g, ctx_out):
    qc, kc, vc, oc, ic, fc = mlstm_inputs(zc, conv_w, conv_b, b_igate, b_fgate)
    qx, kx, vx, ox, ix, fx = mlstm_inputs(zx, conv_w, conv_b, b_igate, b_fgate)
    B = qx.shape[0]
    zero = (jnp.zeros((B, MLSTM_HEADS, MLSTM_DK, MLSTM_DV), jnp.float32),
            jnp.zeros((B, MLSTM_HEADS, MLSTM_DK), jnp.float32),
            jnp.zeros((B, MLSTM_HEADS), jnp.float32))
    hx = jnp.zeros_like(vx)
    hc = jnp.zeros_like(vc)
    for d in range(2):
        rev = (lambda a: jnp.flip(a, axis=1)) if d == 1 else (lambda a: a)
        if ctx_out:
            h, st = mlstm_scan(rev(qc), rev(kc), rev(vc), rev(ic[:, :, d]), rev(fc[:, :, d]), zero)
            hc = hc + rev(h)
        else:
            st = mlstm_final_state(rev(kc), rev(vc), rev(ic[:, :, d]), rev(fc[:, :, d]), zero)
        h, _ = mlstm_scan(rev(qx), rev(kx), rev(vx), rev(ix[:, :, d]), rev(fx[:, :, d]), st)
        hx = hx + rev(h)

    def finish(h, o):
        hn = rms_norm(h, norm_g.reshape(MLSTM_HEADS, MLSTM_DV))
        return hn.reshape(o.shape) * o

    return finish(hx, ox), (finish(hc, oc) if ctx_out else None)


def gmlp_group(z, w_s, b_s, norm_g, n_chunks):
    B, T, _ = z.shape
    u, v = jnp.split(jax.nn.gelu(z), 2, axis=-1)
    v = rms_norm(v, norm_g).reshape(B, n_chunks, T // n_chunks, GMLP_GROUPS, GMLP_GD)
    s = jnp.einsum('gpq,bcqgd->bcpgd', w_s, v) + b_s.T[:, :, None]
    return u * s.reshape(B, T, GMLP_WIDTH)


def setup_inputs(seed: int = 0) -> dict:
    key = jax.random.key(seed)
    ks = jax.random.split(key, 24)

    def nrm(k, shape, scale):
        return jax.random.normal(k, shape, jnp.float32) * scale

    L, D = DEPTH, D_MODEL
    return {
        "x": nrm(ks[0], (BATCH, SEQ, D), 1.0),
        "c": nrm(ks[1], (BATCH, D), 1.0),
        "ctx": nrm(ks[2], (BATCH, CTX_LEN, D), 1.0),
        "c_ctx": nrm(ks[3], (D,), 1.0),
        "w_ada": nrm(ks[4], (L, D, N_MOD * D), 0.5 * D ** -0.5),
        "b_ada": nrm(ks[5], (L, N_MOD * D), 0.02),
        "norm_ffn1": 1.0 + nrm(ks[6], (L, D), 0.02),
        "w_ffn1_in": nrm(ks[7], (L, D, 2 * D_FF), D ** -0.5),
        "w_ffn1_out": nrm(ks[8], (L, D_FF, D), D_FF ** -0.5),
        "norm_mix": 1.0 + nrm(ks[9], (L, D), 0.02),
        "w_in": nrm(ks[10], (L, D, IN_COLS), D ** -0.5),
        "conv_w": nrm(ks[11], (L, CONV_K, QK_WIDTH), CONV_K ** -0.5),
        "conv_b": nrm(ks[12], (L, QK_WIDTH), 0.02),
        "b_igate": nrm(ks[13], (L, 2, MLSTM_HEADS), 0.1),
        "b_fgate": 3.0 + 3.0 * jax.random.uniform(ks[14], (L, 2, MLSTM_HEADS), jnp.float32),
        "mlstm_norm": 1.0 + nrm(ks[15], (L, MLSTM_WIDTH), 0.02),
        "gmlp_norm": 1.0 + nrm(ks[16], (L, GMLP_WIDTH), 0.02),
        "gmlp_w": nrm(ks[17], (L, GMLP_GROUPS, GMLP_CHUNK, GMLP_CHUNK), GMLP_CHUNK ** -0.5),
        "gmlp_b": 1.0 + nrm(ks[18], (L, GMLP_GROUPS, GMLP_CHUNK), 0.1),
        "w_out": nrm(ks[19], (L, MIX_WIDTH, D), MIX_WIDTH ** -0.5),
        "norm_ffn2": 1.0 + nrm(ks[20], (L, D), 0.02),
        "w_ffn2_in": nrm(ks[21], (L, D, 2 * D_FF), D ** -0.5),
        "w_ffn2_out": nrm(ks[22], (L, D_FF, D), D_FF ** -0.5),
        "final_norm": 1.0 + nrm(ks[23], (D,), 0.02),
    }


def reference(x, c, ctx, c_ctx, w_ada, b_ada, norm_ffn1, w_ffn1_in, w_ffn1_out, norm_mix, w_in,
              conv_w, conv_b, b_igate, b_fgate, mlstm_norm, gmlp_norm, gmlp_w, gmlp_b, w_out,
              norm_ffn2, w_ffn2_in, w_ffn2_out, final_norm):
    rows = x.shape[1] // GRID_W
    n_lat_chunks = rows // CHUNK_ROWS
    n_ctx_chunks = ctx.shape[1] // GMLP_CHUNK
    for l in range(DEPTH):
        last = l == DEPTH - 1
        mx = [m[:, None, :] for m in adaln(c, w_ada[l], b_ada[l])]
        mc = adaln(c_ctx, w_ada[l], b_ada[l])

        x = x + 0.5 * mx[2] * swiglu(modulate(rms_norm(x, norm_ffn1[l]), mx[0], mx[1]), w_ffn1_in[l], w_ffn1_out[l])
        ctx = ctx + 0.5 * mc[2] * swiglu(modulate(rms_norm(ctx, norm_ffn1[l]), mc[0], mc[1]), w_ffn1_in[l], w_ffn1_out[l])

        zx = modulate(rms_norm(x, norm_mix[l]), mx[3], mx[4]) @ w_in[l]
        w_in_ctx = w_in[l][:, :COL_GATE_END] if last else w_in[l]
        zc = modulate(rms_norm(ctx, norm_mix[l]), mc[3], mc[4]) @ w_in_ctx
        hx, hc = mlstm_group(zc[..., :COL_GATE_END], zx[..., :COL_GATE_END], conv_w[l], conv_b[l],
                             b_igate[l], b_fgate[l], mlstm_norm[l], not last)
        gx = gmlp_group(zx[..., COL_GATE_END:], gmlp_w[l], gmlp_b[l], gmlp_norm[l], n_lat_chunks)
        x = x + mx[5] * (jnp.concatenate([hx, gx], axis=-1) @ w_out[l])

        x = x + 0.5 * mx[8] * swiglu(modulate(rms_norm(x, norm_ffn2[l]), mx[6], mx[7]), w_ffn2_in[l], w_ffn2_out[l])

        if not last:
            gc = gmlp_group(zc[..., COL_GATE_END:], gmlp_w[l], gmlp_b[l], gmlp_norm[l], n_ctx_chunks)
            ctx = ctx + mc[5] * (jnp.concatenate([hc, gc], axis=-1) @ w_out[l])
            ctx = ctx + 0.5 * mc[8] * swiglu(modulate(rms_norm(ctx, norm_ffn2[l]), mc[6], mc[7]), w_ffn2_in[l], w_ffn2_out[l])
    return rms_norm(x, final_norm)
```

```python
import numpy as np
from contextlib import ExitStack
import concourse.bass as bass
import concourse.mybir as mybir
from concourse.bass_utils import run_bass_kernel_spmd

F32 = mybir.dt.float32
BF16 = mybir.dt.bfloat16
AF = mybir.ActivationFunctionType
ALU = mybir.AluOpType
AX = mybir.AxisListType


class _Key:
    __slots__ = ("w", "r")

    def __init__(self):
        self.w = None
        self.r = {}


class Tracker:
    ENGS = ("pe", "act", "dve", "pool", "sp")

    def __init__(self, sems):
        self.sems = sems
        self.count = {n: 0 for n in sems}
        self.known = {e: {} for e in self.ENGS}
        self.lists = {e: [] for e in self.ENGS}
        self.keys = {}
        self.n_ops = 0
        self.n_waits = 0
        self.alias = {}

    def _k(self, key):
        k = self.keys.get(key)
        if k is None:
            k = self.keys[key] = _Key()
        return k

    def _need(self, reads, writes, need=None):
        if need is None:
            need = {}
        for key in reads:
            k = self.keys.get(key)
            if k is not None and k.w is not None:
                s, v = k.w
                if need.get(s, 0) < v:
                    need[s] = v
        for key in writes:
            k = self.keys.get(key)
            if k is None:
                continue
            if k.w is not None:
                s, v = k.w
                if need.get(s, 0) < v:
                    need[s] = v
            for s, v in k.r.items():
                if need.get(s, 0) < v:
                    need[s] = v
        return need

    def _waits(self, eng, need):
        kn = self.known[eng]
        for s, v in need.items():
            if kn.get(s, 0) >= v:
                continue
            if s == eng:
                assert v <= self.count[s], f"self-wait on future signal {eng} {v}>{self.count[s]}"
            kn[s] = v
            self.lists[eng].append(("w", self.sems[s], v))
            self.n_waits += 1

    def _stamp(self, reads, writes, stamp):
        s, v = stamp
        for key in reads:
            k = self._k(key)
            if k.r.get(s, 0) < v:
                k.r[s] = v
        for key in writes:
            k = self._k(key)
            k.w = stamp
            k.r = {}

    def op(self, eng, fn, reads=(), writes=()):
        need = self._need(reads, writes)
        self._waits(eng, need)
        self.count[eng] += 1
        self.lists[eng].append(("i", fn, self.sems[eng], 1))
        self._stamp(reads, writes, (eng, self.count[eng]))
        self.n_ops += 1

    def mmgroup(self, steps, writes, eng="pe"):
        n = len(steps)
        for i, (fn, reads) in enumerate(steps):
            need = self._need(reads, writes if i == 0 else ())
            self._waits(eng, need)
            if i == n - 1:
                self.count[eng] += 1
                self.lists[eng].append(("i", fn, self.sems[eng], 1))
            else:
                self.lists[eng].append(("i", fn, None, 0))
            self.n_ops += 1
        stamp = (eng, self.count[eng])
        for fn, reads in steps:
            self._stamp(reads, (), stamp)
        self._stamp((), writes, stamp)

    def new_stage(self):
        self.alias = {}

    def dma(self, eng, sem, dmas, wait_prev=True):
        if not sem.startswith("cv"):
            a = self.alias.get(sem)
            if a is None:
                a = self.alias[sem] = f"g{len(self.alias)}"
                assert a in self.sems, "out of generic DMA semaphores"
            sem = a
        need = {}
        for fn, reads, writes in dmas:
            self._need(reads, writes, need)
        if wait_prev and self.count[sem] > 0 and need.get(sem, 0) < self.count[sem]:
            need[sem] = self.count[sem]
        self._waits(eng, need)
        final = self.count[sem] + 16 * len(dmas)
        for fn, reads, writes in dmas:
            self.lists[eng].append(("i", fn, self.sems[sem], 16))
            self._stamp(reads, writes, (sem, final))
            self.n_ops += 1
        self.count[sem] = final

    def barrier(self):
        need = {s: v for s, v in self.count.items() if v > 0 and not s.startswith("cv")}
        for e in self.ENGS:
            self._waits(e, dict(need))

    def replay(self, eng, bass_eng, start):
        lst = self.lists[eng]
        for item in lst[start:]:
            if item[0] == "w":
                bass_eng.wait_ge(item[1], item[2])
            else:
                ins = item[1](bass_eng)
                if item[2] is not None:
                    ins.then_inc(item[2], item[3])
        return len(lst)


class Cfg:
    def __init__(self, D=2048, DFF=5632, SEQ=4096, CTX=256, TT=512, debug=False, stages=99):
        self.D, self.DFF, self.SEQ, self.CTX, self.TT = D, DFF, SEQ, CTX, TT
        self.NC = D // 128
        self.NK = DFF // 128
        self.NXT = SEQ // TT
        self.debug = debug
        self.stages = stages
        self.tiles = [("c", 0, 0, CTX)] + [("x", t, t * TT, TT) for t in range(self.NXT)]
        self.NADA = 9 * D // 512
        self.EPS = 1e-6


N_CVT_SEMS = 8
N_GEN_SEMS = 28
DMA_SEMS = [f"g{i}" for i in range(N_GEN_SEMS)] + [f"cv{i}" for i in range(N_CVT_SEMS)]


COLNAMES = ["A1x", "SH1x", "HG1x", "A2x", "SH2x", "G2x", "A3x", "SH3x", "HG3x", "A1c", "SH1c", "HG1c",
            "A2c", "SH2c", "FN"]


def MM(o, l, r, st, sp):
    return lambda e: e.matmul(o, lhsT=l, rhs=r, start=st, stop=sp)


def TR(o, i, ident):
    return lambda e: e.transpose(o, i, ident)


def ACT(o, i, func, bias=None, scale=None):
    kw = {}
    if bias is not None:
        kw["bias"] = bias
    if scale is not None:
        kw["scale"] = scale
    return lambda e: e.activation(out=o, in_=i, func=func, **kw)


def TT_(o, a, b, op):
    return lambda e: e.tensor_tensor(out=o, in0=a, in1=b, op=op)


def STT(o, a, s, b, op0, op1):
    return lambda e: e.scalar_tensor_tensor(out=o, in0=a, scalar=s, in1=b, op0=op0, op1=op1)


def TS(o, a, s1, op0, s2=None, op1=None):
    if op1 is None:
        return lambda e: e.tensor_scalar(out=o, in0=a, scalar1=s1, scalar2=None, op0=op0)
    return lambda e: e.tensor_scalar(out=o, in0=a, scalar1=s1, scalar2=s2, op0=op0, op1=op1)


def CP(o, i):
    return lambda e: e.tensor_copy(out=o, in_=i)


def RCP(o, i):
    return lambda e: e.reciprocal(out=o, in_=i)


def MSET(o, v):
    return lambda e: e.memset(o, v)


def DMA(o, i):
    return lambda e: e.dma_start(out=o, in_=i)


class SlabStream:
    def __init__(self, T, name, bufs, sems, uses):
        self.T, self.name, self.bufs, self.sems, self.uses = T, name, bufs, sems, uses
        self.nxt = 0

    def _issue(self, v):
        src, deps = self.uses[v]
        sl = v % len(self.bufs)
        self.T.dma("sp", self.sems[sl], [(DMA(self.bufs[sl][:], src), deps, [(self.name, sl)])])

    def get(self, u):
        d = len(self.bufs)
        while self.nxt < len(self.uses) and self.nxt <= u + d - 1:
            self._issue(self.nxt)
            self.nxt += 1
        sl = u % d
        return self.bufs[sl], (self.name, sl)

    def prefetch(self, u):
        self.get(u)


class Ring:
    def __init__(self, name, bufs):
        self.name, self.bufs, self.i = name, bufs, 0

    def next(self):
        k = self.i % len(self.bufs)
        self.i += 1
        return self.bufs[k], (self.name, k), k


class Builder:
    def __init__(self, cfg):
        self.cfg = cfg
        self.nc = bass.Bass("TRN2", target_bir_lowering=False)
        self.es = ExitStack()

    def din(self, name, shape, dt=F32):
        return self.nc.dram_tensor(name, list(shape), dt, kind="ExternalInput").ap()

    def dscr(self, name, shape, dt):
        return self.nc.dram_tensor(name, list(shape), dt, kind="Internal").ap()

    def dout(self, name, shape, dt=F32):
        return self.nc.dram_tensor(name, list(shape), dt, kind="ExternalOutput").ap()

    def dbg(self, name, shape, dt):
        return self.dout(name, shape, dt) if self.cfg.debug else self.dscr(name, shape, dt)

    def run_block(self):
        T, pos, nc = self.T, self.pos, self.nc
        T.new_stage()
        with nc.Block() as block:
            @block.sync
            def _(e):
                pos["sp"] = T.replay("sp", e, pos["sp"])

            @block.tensor
            def _(e):
                pos["pe"] = T.replay("pe", e, pos["pe"])

            @block.scalar
            def _(e):
                pos["act"] = T.replay("act", e, pos["act"])

            @block.vector
            def _(e):
                pos["dve"] = T.replay("dve", e, pos["dve"])

            @block.gpsimd
            def _(e):
                pos["pool"] = T.replay("pool", e, pos["pool"])

    def col(self, name, j):
        o = COLNAMES.index(name) * self.cfg.NC + j
        return self.COLS[:, o:o + 1]

    def colblk(self, name):
        o = COLNAMES.index(name) * self.cfg.NC
        return self.COLS[:, o:o + self.cfg.NC]

    def convert(self, name, src, dst, nslab, per_dma, extra_reads=()):
        T = self.T
        slab_elems = 1
        for s in src.shape[1:]:
            slab_elems *= s
        assert slab_elems % 2048 == 0
        nd = len(src.shape)
        names = " ".join(f"a{i}" for i in range(1, nd))
        s2 = src.rearrange(f"s {names} -> s ({names})").rearrange("s (r n) -> s r n", n=2048)
        d2 = dst.rearrange(f"s {names} -> s ({names})").rearrange("s (r n) -> s r n", n=2048)
        for g0 in range(0, nslab, per_dma):
            g1 = min(nslab, g0 + per_dma)
            sem = f"cv{self.cvt_i % N_CVT_SEMS}"
            self.cvt_i += 1
            si = s2[g0:g1].rearrange("s r n -> (s r) n")
            do = d2[g0:g1].rearrange("s r n -> (s r) n")
            keys = [(name, s) for s in range(g0, g1)]
            T.dma("pool", sem, [(DMA(do, si), tuple(extra_reads), keys)])

    def build(self):
        cfg, nc, es = self.cfg, self.nc, self.es
        D, NC, NK, TT = cfg.D, cfg.NC, cfg.NK, cfg.TT
        self.xT = self.din("xT", [D, cfg.SEQ])
        self.ctxT = self.din("ctxT", [D, cfg.CTX])
        self.cc = self.din("cc", [128, NC, 2])
        self.w_ada_h = self.din("w_ada_h", [cfg.NADA, 128, NC, 512])
        self.b_ada_h = self.din("b_ada_h", [128, 9 * NC])
        self.nrm_h = self.din("nrm_h", [128, 4, NC])
        self.w1in_h = self.din("w1in_h", [NK, 128, NC, 2, 128])
        self.w1out_h = self.din("w1out_h", [NC, 128, NK, 128])
        self.w1in_b = self.dscr("w1in_b", [NK, 128, NC, 2, 128], BF16)
        self.w1out_b = self.dscr("w1out_b", [NC, 128, NK, 128], BF16)
        NTILES = len(cfg.tiles)
        self.x1T = self.dbg("x1T", [D, cfg.SEQ], F32)
        self.xn2 = self.dbg("xn2", [NTILES, 128, NC, TT], BF16)
        if cfg.debug:
            self.cols_out = self.dout("cols_out", [128, len(COLNAMES) * NC])
        if cfg.stages >= 2:
            self.declare_mix()
        if cfg.stages >= 6:
            self.w2in_h = self.din("w2in_h", [NK, 128, NC, 2, 128])
            self.w2out_h = self.din("w2out_h", [NC, 128, NK, 128])
            self.w2in_b = self.dscr("w2in_b", [NK, 128, NC, 2, 128], BF16)
            self.w2out_b = self.dscr("w2out_b", [NC, 128, NK, 128], BF16)
            self.oT = self.dout("oT", [D, cfg.SEQ])

        with es:
            sems = {}
            for n in list(Tracker.ENGS) + DMA_SEMS:
                sems[n] = es.enter_context(nc.semaphore(n))
            self.T = Tracker(sems)
            self.pos = {e: 0 for e in Tracker.ENGS}
            self.cvt_i = 0
            self.COLS = es.enter_context(nc.sbuf_tensor("COLS", [128, len(COLNAMES) * NC], F32))
            self.ONES = es.enter_context(nc.sbuf_tensor("ONES", [128, 128], F32))

            self.convert("w1in_b", self.w1in_h, self.w1in_b, NK, 8)
            if cfg.stages >= 2:
                self.GT = es.enter_context(nc.sbuf_tensor("GT", [128, self.NCH, 16], F32))
                for nm in ("WCOL", "THR", "DEC", "DECQ"):
                    setattr(self, nm, es.enter_context(nc.sbuf_tensor(nm, [128, 2, self.NCH, 4], F32)))
                self.MASK = [es.enter_context(nc.sbuf_tensor(f"MASK{d}", [128, 128], F32)) for d in range(2)]
                self.IDB = es.enter_context(nc.sbuf_tensor("IDB", [128, 128], BF16))
            self.stage0()
            self.ffn_stage("ffn1")
            if cfg.stages >= 2:
                self.stage2()
            if cfg.stages >= 3:
                self.stage3()
            if cfg.stages >= 4:
                self.scan_both()
            if cfg.stages >= 5:
                self.out_stage()
            if cfg.stages >= 6:
                self.ffn_stage("ffn2")
        return nc


    def declare_mix(self):
        cfg = self.cfg
        D, NC, SEQ, CTX = cfg.D, cfg.NC, cfg.SEQ, cfg.CTX
        self.NCHC, self.NCHX = CTX // 128, SEQ // 128
        self.NCH = self.NCHC + self.NCHX
        self.win_fm_h = self.din("win_fm_h", [12, 128, NC, 256])
        self.win_tm_h = self.din("win_tm_h", [4, 128, NC, 512])
        self.wg_h = self.din("wg_h", [1, 128, NC, 16])
        self.win_fm_b = self.dscr("win_fm_b", [12, 128, NC, 256], BF16)
        self.win_tm_b = self.dscr("win_tm_b", [4, 128, NC, 512], BF16)
        self.wg_b = self.dscr("wg_b", [1, 128, NC, 16], BF16)
        self.convw_h = self.din("convw_h", [128, 8, 6])
        self.gbias_h = self.din("gbias_h", [16])
        self.gnorm_h = self.din("gnorm_h", [1024])
        self.qk_x = self.dbg("qk_x", [128, 8, SEQ], BF16)
        self.k_c = self.dbg("k_c", [128, 4, CTX], BF16)
        self.vext_s = self.dbg("vext_s", [self.NCH, 128, 1028], BF16)
        self.oT_s = self.dbg("oT_s", [128, 8, SEQ], BF16)
        self.uT_s = self.dbg("uT_s", [128, 8, SEQ], BF16)
        self.vg_s = self.dbg("vg_s", [self.NCHX, 128, 1024], BF16)
        self.hfwd_s = self.dbg("hfwd_s", [self.NCHX, 128, 1024], F32)
        self.hbwd_s = self.dbg("hbwd_s", [self.NCHX, 128, 1024], F32)
        if cfg.stages >= 5:
            self.wsT_h = self.din("wsT_h", [128, 8, 128])
            self.gmlpb_h = self.din("gmlpb_h", [1024])
            self.mnorm_h = self.din("mnorm_h", [128, 8])
            self.wout_h = self.din("wout_h", [NC, 128, NC, 128])
            self.wout_b = self.dscr("wout_b", [NC, 128, NC, 128], BF16)
            self.x2T = self.dbg("x2T", [D, SEQ], F32)
        if cfg.debug:
            self.gt_out = self.dout("gt_out", [128, self.NCH, 16])
            self.g3_out = self.dout("g3_out", [128, 4, 2 * self.NCH * 4])

    def stage2(self):
        cfg, nc, T = self.cfg, self.nc, self.T
        NC, TT = cfg.NC, cfg.TT
        tiles = cfg.tiles
        GT = self.GT
        with ExitStack() as s2:
            def sb(name, shape, dt):
                return s2.enter_context(nc.sbuf_tensor(name, shape, dt))

            def ps(name):
                return s2.enter_context(nc.psum_tensor(name, [128, 512], F32))

            XN2 = [sb(f"XN2_{i}", [128, NC, TT], BF16) for i in range(2)]
            FM = [sb(f"FM{i}", [128, NC, 256], BF16) for i in range(3)]
            TM = [sb(f"TM{i}", [128, NC, 512], BF16) for i in range(2)]
            WG = sb("WG", [128, NC, 16], BF16)
            ZQK = sb("ZQK", [128, 8, TT + 6], F32)
            CACC = Ring("CACC", [sb(f"CACC{i}", [128, TT + 2], F32) for i in range(2)])
            QKB = Ring("QKB", [sb(f"QKB{i}", [128, TT + 2], BF16) for i in range(2)])
            STGB = Ring("STGB", [sb(f"STGB{i}", [128, TT], BF16) for i in range(4)])
            VROW = [sb(f"VROW{i}", [128, 4, 257], BF16) for i in range(8)]
            VGF = [sb(f"VGF{i}", [128, 1024], F32) for i in range(4)]
            VGB = Ring("VGB", [sb(f"VGB{i}", [128, 1024], BF16) for i in range(2)])
            SQJ = sb("SQJ", [128, 1024], BF16)
            SSQ = sb("SSQ", [128, 8], F32)
            CW = sb("CW", [128, 8, 6], F32)
            GB16 = sb("GB16", [128, 16], F32)
            GNB = sb("GNB", [128, 1024], F32)
            PF = [ps(f"PF{i}") for i in range(2)]
            PT = [ps(f"PT{i}") for i in range(2)]
            PGt = [ps(f"PGt{i}") for i in range(2)]

            T.dma("sp", "misc", [
                (DMA(CW[:], self.convw_h), (), ["CW"]),
                (DMA(GB16[:], self.gbias_h.partition_broadcast(128)), (), ["GB16"]),
                (DMA(GNB[:], self.gnorm_h.partition_broadcast(128)), (), ["GNB"]),
                (DMA(WG[:], self.wg_b[0]), [("wg_b", 0)], ["WG"]),
            ])
            T.op("dve", MSET(ZQK[:], 0.0), (), [("ZQK", j) for j in range(8)] + ["ZQKc"])
            for i in range(8):
                T.op("pool", MSET(VROW[i][:, :, 256:257], 1.0), (), [("VROW", i)])

            if cfg.stages >= 5:
                self.convert("wout_b", self.wout_h, self.wout_b, NC, 8)
            if cfg.stages >= 6:
                self.convert("w2in_b", self.w2in_h, self.w2in_b, cfg.NK, 8)
                self.convert("w2out_b", self.w2out_h, self.w2out_b, NC, 4)
            fm_uses, tm_uses = [], []
            for (kind, idx, off, nt) in tiles:
                for sidx in (range(12) if kind == "x" else (2, 3)):
                    fm_uses.append((self.win_fm_b[sidx], [("win_fm_b", sidx)]))
                for sidx in (range(4) if kind == "x" else (0, 1)):
                    tm_uses.append((self.win_tm_b[sidx], [("win_tm_b", sidx)]))
            FMS = SlabStream(T, "FM", FM, ["fm0", "fm1", "fm2"], fm_uses)
            TMS = SlabStream(T, "TM", TM, ["tm0", "tm1"], tm_uses)

            def load_xn2(ti):
                kind, idx, off, nt = tiles[ti]
                b = ti % 2
                T.dma("sp", f"xn{b}", [(DMA(XN2[b][:, :, :nt], self.xn2[ti, :, :, :nt]), [("xn2", ti)],
                                         [("XN2", b)])])

            load_xn2(0)
            ufm = utm = 0
            nvr = 0
            ngt = 0
            for ti, (kind, idx, off, nt) in enumerate(tiles):
                b = ti % 2
                isx = kind == "x"
                if ti + 1 < len(tiles):
                    load_xn2(ti + 1)
                nblk = nt // 128
                cg0 = off // 128 + (self.NCHC if isx else 0)
                last_x = isx and idx == cfg.NXT - 1
                for sidx in (range(12) if isx else (2, 3)):
                    W, wkey = FMS.get(ufm)
                    ufm += 1
                    for h in range(2):
                        pb = (2 * sidx + h) % 2
                        steps = [(MM(PF[pb][:, :nt], W[:, c, h * 128:(h + 1) * 128], XN2[b][:, c, :nt],
                                     c == 0, c == NC - 1), [wkey, ("XN2", b)]) for c in range(NC)]
                        T.mmgroup(steps, [("PF", pb)])
                        if sidx < 4:
                            j = sidx * 2 + h
                            T.op("dve", CP(ZQK[:, j, 4:4 + nt], PF[pb][:, :nt]), [("PF", pb)], [("ZQK", j)])
                        else:
                            j = ((sidx - 4) % 4) * 2 + h
                            func = AF.Sigmoid if sidx < 8 else AF.Gelu_apprx_tanh
                            dst = (self.oT_s if sidx < 8 else self.uT_s)[:, j, off:off + nt]
                            sg, sgk, k = STGB.next()
                            T.op("act", ACT(sg[:, :nt], PF[pb][:, :nt], func), [("PF", pb)], [sgk])
                            T.dma("sp", f"so{k}", [(DMA(dst, sg[:, :nt]), [sgk],
                                                     [("oT_s" if sidx < 8 else "uT_s", idx)])])
                    if sidx == 3:
                        i0 = 2 if (not isx or idx == 0) else 0
                        i1 = nt + 2 if (not isx or last_x) else nt
                        Wd = i1 - i0
                        for j in (range(8) if isx else range(4, 8)):
                            acc, acck, _ = CACC.next()
                            z = ZQK[:, j, :]
                            T.op("dve", TS(acc[:, :Wd], z[:, i0:i1], CW[:, j, 0:1], ALU.mult, CW[:, j, 5:6], ALU.add),
                                 [("ZQK", j), "ZQKc", "CW"], [acck])
                            for jj in range(1, 5):
                                T.op("dve", STT(acc[:, :Wd], z[:, i0 + jj:i1 + jj], CW[:, j, jj:jj + 1], acc[:, :Wd],
                                                ALU.mult, ALU.add), [("ZQK", j), "ZQKc", "CW", acck], [acck])
                            qb, qbk, k = QKB.next()
                            T.op("act", ACT(qb[:, :Wd], acc[:, :Wd], AF.Silu), [acck], [qbk])
                            if isx:
                                dst = self.qk_x[:, j, off - 2 + i0:off - 2 + i1]
                            else:
                                dst = self.k_c[:, j - 4, i0 - 2:i1 - 2]
                            T.dma("sp", f"sq{k}", [(DMA(dst, qb[:, :Wd]), [qbk], [("qk_s", kind, idx)])])
                        if isx and not last_x:
                            T.op("dve", CP(ZQK[:, :, 0:4], ZQK[:, :, nt:nt + 4]),
                                 [("ZQK", j) for j in range(8)], ["ZQKc"])
                vr_of = {}
                for sidx in (range(4) if isx else (0, 1)):
                    W, wkey = TMS.get(utm)
                    utm += 1
                    for blk in range(nblk):
                        pb = (sidx * nblk + blk) % 2
                        steps = [(MM(PT[pb][:, :], XN2[b][:, c, blk * 128:(blk + 1) * 128], W[:, c, :],
                                     c == 0, c == NC - 1), [wkey, ("XN2", b)]) for c in range(NC)]
                        T.mmgroup(steps, [("PT", pb)])
                        if sidx < 2:
                            if sidx == 0:
                                vr_of[blk] = nvr % 8
                                nvr += 1
                            r = vr_of[blk]
                            T.op("dve", CP(VROW[r][:, 2 * sidx:2 * sidx + 2, 0:256],
                                           PT[pb][:, :].rearrange("p (h e) -> p h e", h=2)),
                                 [("PT", pb)], [("VROW", r)])
                            if sidx == 1:
                                T.dma("sp", f"vr{r}", [(DMA(self.vext_s[cg0 + blk], VROW[r][:].rearrange("p h e -> p (h e)")),
                                                        [("VROW", r)], [("vext_s", cg0 + blk)])])
                        else:
                            hv = sidx - 2
                            T.op("act", ACT(VGF[blk][:, hv * 512:(hv + 1) * 512], PT[pb][:, :], AF.Gelu_apprx_tanh),
                                 [("PT", pb)], [("VGF", blk, hv)])
                            if hv == 1:
                                T.op("act", lambda e, blk=blk: e.activation(
                                    out=SQJ[:], in_=VGF[blk][:], func=AF.Square, accum_out=SSQ[:, blk:blk + 1]),
                                    [("VGF", blk, 0), ("VGF", blk, 1)], ["SQJ", ("SSQ", blk)])
                                T.op("act", ACT(SSQ[:, 4 + blk:5 + blk], SSQ[:, blk:blk + 1], AF.Sqrt,
                                                bias=cfg.EPS, scale=1.0 / 1024), [("SSQ", blk)], [("SSR", blk)])
                                T.op("dve", RCP(SSQ[:, 4 + blk:5 + blk], SSQ[:, 4 + blk:5 + blk]),
                                     [("SSR", blk)], [("SSR", blk)])
                                vb, vbk, k = VGB.next()
                                T.op("dve", STT(vb[:], VGF[blk][:], SSQ[:, 4 + blk:5 + blk], GNB[:], ALU.mult, ALU.mult),
                                     [("VGF", blk, 0), ("VGF", blk, 1), ("SSR", blk), "GNB"], [vbk])
                                T.dma("sp", f"vg{k}", [(DMA(self.vg_s[cg0 - self.NCHC + blk], vb[:]), [vbk],
                                                         [("vg_s", cg0 - self.NCHC + blk)])])
                for blk in range(nblk):
                    pb = ngt % 2
                    ngt += 1
                    steps = [(MM(PGt[pb][:, 0:16], XN2[b][:, c, blk * 128:(blk + 1) * 128], WG[:, c, :],
                                 c == 0, c == NC - 1), ["WG", ("XN2", b)]) for c in range(NC)]
                    T.mmgroup(steps, [("PGt", pb)])
                    T.op("dve", TT_(GT[:, cg0 + blk, :], PGt[pb][:, 0:16], GB16[:], ALU.add),
                         [("PGt", pb), "GB16"], [("GT", cg0 + blk)])
            if cfg.debug:
                T.dma("sp", "misc", [(DMA(self.gt_out, GT[:]), [("GT", c) for c in range(self.NCH)], ())])
            T.barrier()
            self.run_block()


    def stage3(self):
        cfg, nc, T = self.cfg, self.nc, self.T
        NCH, NCHC = self.NCH, self.NCHC
        NQ = NCH * 4
        assert NQ % 2 == 0 and NQ // 2 <= 128 and 2 * NQ <= 512
        HQ = NQ // 2
        GT, ONES = self.GT, self.ONES
        with ExitStack() as s3:
            def sb(name, shape, dt=F32):
                return s3.enter_context(nc.sbuf_tensor(name, shape, dt))

            def ps(name):
                return s3.enter_context(nc.psum_tensor(name, [128, 512], F32))

            LI = sb("LI", [128, 2, NCH, 4]); ZF = sb("ZF", [128, 2, NCH, 4]); AB = sb("AB", [128, 2 * NQ])
            EX = sb("EX", [128, 2 * NQ]); LN = sb("LN", [128, 2 * NQ]); MN = sb("MN", [128, 2 * NQ])
            LF = sb("LF", [128, 2, NQ]); Bs = sb("Bs", [128, 2, NQ]); TOT = sb("TOT", [128, 2, NQ])
            A = sb("A", [128, 2, NQ]); AMX = sb("AMX", [128, 4]); AMR = sb("AMR", [1, 2, NCH, 4])
            MROW = sb("MROW", [1, 2, NCH, 4]); MOUT = sb("MOUT", [1, 2, NCH, 4]); MINR = sb("MINR", [1, 2, NCH, 4])
            ZERO4 = sb("ZERO4", [1, 4]); MBC = sb("MBC", [128, 2 * NQ]); T1 = sb("T1", [128, 2 * NQ])
            T2 = sb("T2", [128, 2 * NQ]); T3 = sb("T3", [128, 2 * NQ])
            TRI = [sb("TRIF", [128, 128]), sb("TRIB", [128, 128])]
            IDF = sb("IDF", [128, 128])
            PB = [ps("PB0"), ps("PB1")]
            PBt = [ps("PBt0"), ps("PBt1")]
            PX = ps("PX"); PR = ps("PR"); PM = ps("PM"); PN = ps("PN")
            GTv = GT[:].rearrange("p c (d f h) -> p c d f h", d=2, f=2, h=4)
            gtk = [("GT", c) for c in range(NCH)]

            for d in range(2):
                T.op("pool", MSET(TRI[d][:], 1.0), (), [("TRI", d)])
                T.op("pool", lambda e, d=d: e.affine_select(
                    out=TRI[d][:], in_=TRI[d][:], pattern=[[1 if d == 0 else -1, 128]], compare_op=ALU.is_ge,
                    fill=0.0, base=0, channel_multiplier=(-1 if d == 0 else 1)), [("TRI", d)], [("TRI", d)])
                T.op("dve", TS(self.MASK[d][:], TRI[d][:], 128.0 ** -0.5, ALU.mult), [("TRI", d)], [("MASK", d)])
            T.op("dve", TT_(IDF[:], TRI[0][:], TRI[1][:], ALU.mult), [("TRI", 0), ("TRI", 1)], ["IDF"])
            T.op("dve", CP(self.IDB[:], IDF[:]), ["IDF"], ["IDB"])

            for d in range(2):
                T.op("dve", CP(LI[:, d], GTv[:, :, d, 0, :]), gtk, ["LI"])
                T.op("dve", CP(ZF[:, d], GTv[:, :, d, 1, :]), gtk, ["ZF"])
            zf = ZF[:].rearrange("p d c h -> p (d c h)")
            T.op("act", ACT(AB[:], zf, AF.Abs), ["ZF"], ["AB"])
            T.op("act", ACT(EX[:], AB[:], AF.Exp, scale=-1.0), ["AB"], ["EX"])
            T.op("act", ACT(LN[:], EX[:], AF.Ln, bias=1.0), ["EX"], ["LN"])
            T.op("dve", TS(MN[:], zf, 0.0, ALU.min), ["ZF"], ["MN"])
            T.op("dve", TT_(LF[:].rearrange("p d q -> p (d q)"), MN[:], LN[:], ALU.subtract), ["MN", "LN"], ["LF"])
            for d in range(2):
                T.mmgroup([(MM(PB[d][:, 0:NQ], TRI[d][:], LF[:, d, :], True, True), [("TRI", d), "LF"])],
                          [("PB", d, 0)])
                T.mmgroup([(MM(PBt[d][:, 0:NQ], ONES[:], LF[:, d, :], True, True), ["ONES", "LF"])],
                          [("PB", d, 1)])
                T.op("dve", CP(Bs[:, d, :], PB[d][:, 0:NQ]), [("PB", d, 0)], ["Bs"])
                T.op("dve", CP(TOT[:, d, :], PBt[d][:, 0:NQ]), [("PB", d, 1)], ["TOT"])
            T.op("dve", TT_(A[:].rearrange("p d q -> p (d q)"), LI[:].rearrange("p d c h -> p (d c h)"),
                            Bs[:].rearrange("p d q -> p (d q)"), ALU.subtract), ["LI", "Bs"], ["A"])
            for q in range(4):
                d, hf = q // 2, q % 2
                T.mmgroup([(TR(PX[0:HQ, q * 128:(q + 1) * 128], A[:, d, hf * HQ:(hf + 1) * HQ], IDF[:]),
                            ["A", "IDF"])], [("PX", q)])
            T.op("dve", lambda e: e.tensor_reduce(out=AMX[0:HQ, :], in_=PX[0:HQ, :].rearrange("p (q n) -> p q n", q=4),
                                                  axis=AX.X, op=ALU.max), [("PX", q) for q in range(4)], ["AMX"])
            for q in range(4):
                T.mmgroup([(TR(PR[0:1, q * HQ:(q + 1) * HQ], AMX[0:HQ, q:q + 1], IDF[0:HQ, 0:HQ]), ["AMX", "IDF"])],
                          [("PR", q)])
            T.op("dve", CP(AMR[:].rearrange("p d c h -> p (d c h)"), PR[0:1, 0:2 * NQ]),
                 [("PR", q) for q in range(4)], ["AMR"])
            T.op("dve", MSET(ZERO4[:], 0.0), (), ["ZERO4"])
            T.op("dve", MSET(MINR[:], 0.0), (), ["MINR"])
            order = [list(range(NCH)), list(range(NCHC - 1, -1, -1)) + list(range(NCH - 1, NCHC - 1, -1))]
            self.chunk_order = order
            cur = [ZERO4[:], ZERO4[:]]
            curk = ["ZERO4", "ZERO4"]
            for i in range(NCH):
                for d in range(2):
                    c = order[d][i]
                    T.op("dve", TT_(MROW[0:1, d, c, :], cur[d], AMR[0:1, d, c, :], ALU.max),
                         [curk[d], "AMR"], [("MROW", d, c)])
                    T.op("dve", TT_(MOUT[0:1, d, c, :], TOT[0:1, d, c * 4:(c + 1) * 4], MROW[0:1, d, c, :], ALU.add),
                         ["TOT", ("MROW", d, c)], [("MOUT", d, c)])
                    if i + 1 < NCH:
                        cn = order[d][i + 1]
                        T.op("pool", CP(MINR[0:1, d, cn, :], MOUT[0:1, d, c, :]), [("MOUT", d, c), "MINR"],
                             [("MINRc", d, cn)])
                    cur[d], curk[d] = MOUT[0:1, d, c, :], ("MOUT", d, c)
            allm = [("MROW", d, c) for d in range(2) for c in range(NCH)]
            allmin = [("MINRc", d, c) for d in range(2) for c in range(NCH)] + ["MINR"]
            T.mmgroup([(MM(PM[:, 0:2 * NQ], ONES[0:1, :], MROW[:].rearrange("p d c h -> p (d c h)"), True, True),
                        allm + ["ONES"])], ["PM"])
            T.mmgroup([(MM(PN[:, 0:2 * NQ], ONES[0:1, :], MINR[:].rearrange("p d c h -> p (d c h)"), True, True),
                        allmin + ["ONES"])], ["PN"])
            T.op("dve", CP(MBC[:], PM[:, 0:2 * NQ]), ["PM"], ["MBC"])
            fl = lambda t: t[:].rearrange("p d c h -> p (d c h)")
            T.op("dve", TT_(T1[:], A[:].rearrange("p d q -> p (d q)"), MBC[:], ALU.subtract), ["A", "MBC"], ["T1"])
            T.op("act", ACT(fl(self.WCOL), T1[:], AF.Exp), ["T1"], ["WCOL"])
            T.op("dve", TT_(T2[:], Bs[:].rearrange("p d q -> p (d q)"), MBC[:], ALU.add), ["Bs", "MBC"], ["T2"])
            T.op("act", ACT(fl(self.THR), T2[:], AF.Exp, scale=-1.0), ["T2"], ["THR"])
            T.op("dve", TT_(T3[:], PN[:, 0:2 * NQ], MBC[:], ALU.subtract), ["PN", "MBC"], ["T3"])
            T.op("act", ACT(fl(self.DEC), T3[:], AF.Exp), ["T3"], ["DEC"])
            T.op("dve", TS(fl(self.DECQ), fl(self.DEC), 128.0 ** -0.5, ALU.mult), ["DEC"], ["DECQ"])
            if cfg.debug:
                T.dma("sp", "misc", [
                    (DMA(self.g3_out[:, 0], fl(self.WCOL)), ["WCOL"], ()),
                    (DMA(self.g3_out[:, 1], fl(self.THR)), ["THR"], ()),
                    (DMA(self.g3_out[:, 2], fl(self.DEC)), ["DEC"], ()),
                    (DMA(self.g3_out[:, 3], fl(self.DECQ)), ["DECQ"], ()),
                ])
            T.barrier()
            self.run_block()


    def scan_stage(self, d):
        cfg, nc, T = self.cfg, self.nc, self.T
        NC, TT = cfg.NC, cfg.TT
        NCH, NCHC, NCHX = self.NCH, self.NCHC, self.NCHX
        order = self.chunk_order[d]
        WCOL, THR, DEC, DECQ, MASK, IDB = self.WCOL, self.THR, self.DEC, self.DECQ, self.MASK, self.IDB
        col = self.col
        nbuf = 2 if d == 0 else 1
        with ExitStack() as s4:
            def sb(name, shape, dt=F32):
                return s4.enter_context(nc.sbuf_tensor(f"{name}_d{d}", shape, dt))

            def ps(name, dt=F32, n=512):
                return s4.enter_context(nc.psum_tensor(f"{name}_d{d}", [128, n], dt))

            QKT = [sb(f"QKT{i}", [128, 8, TT], BF16) for i in range(2)]
            KC = sb("KC", [128, 4, cfg.CTX], BF16)
            VL = Ring("VL", [sb(f"VL{i}", [128, 4, 257], BF16) for i in range(3)])
            C = [sb(f"C{h}", [128, 257]) for h in range(4)]
            Cb = [sb(f"Cb{h}", [128, 257], BF16) for h in range(4)]
            KW = Ring("KW", [sb(f"KW{i}", [128, 128], BF16) for i in range(2)])
            PTT = Ring("PTT", [sb(f"PTT{i}", [128, 128], BF16) for i in range(2)])
            QD = Ring("QD", [sb(f"QD{i}", [128, 128], BF16) for i in range(2)])
            DN = Ring("DN", [sb(f"DN{i}", [128, 4]) for i in range(4)])
            HFS = Ring("HFS", [sb(f"HFS{i}", [128, 4, 256]) for i in range(2)])
            PO = Ring("PO", [ps(f"PO{i}") for i in range(2)])
            PU = Ring("PU", [ps(f"PUs{i}") for i in range(nbuf)])
            PST = Ring("PST", [ps(f"PST{i}") for i in range(nbuf)])
            PK = Ring("PK", [ps(f"PK{i}", BF16, 1024) for i in range(nbuf)])
            if d == 1:
                HSUM = Ring("HSUM", [sb(f"HSUM{i}", [128, 4, 256]) for i in range(2)])
                HN = Ring("HN", [sb(f"HN{i}", [128, 1024], BF16) for i in range(2)])
                SGO = Ring("SGO", [sb(f"SGO{i}", [128, 8, 128], BF16) for i in range(2)])
                UTL = Ring("UTL", [sb(f"UTL{i}", [128, 8, 128], BF16) for i in range(2)])
                VGL = Ring("VGL", [sb(f"VGL{i}", [128, 1024], BF16) for i in range(2)])
                WSTf = sb("WSTf", [128, 8, 128]); WST = sb("WST", [128, 8, 128], BF16)
                BSB = sb("BSB", [128, 8, 128]); MNC = sb("MNC", [128, 8])
                MIXT = [sb(f"MIXT{i}", [128, 16, TT], BF16) for i in range(2)]
                WX = [sb(f"WX{i}", [128, NC, 128], BF16) for i in range(3)]
                X1L = Ring("X1L", [sb(f"X1L{i}", [128, TT]) for i in range(3)])
                X2S = Ring("X2S", [sb(f"X2S{i}", [128, TT]) for i in range(3)])
                TG = Ring("TG", [sb(f"TG{i}", [128, 128]) for i in range(2)])
                SS4 = Ring("SS4", [sb(f"SS4{i}", [128, 12]) for i in range(2)])
                SQJ2 = sb("SQJ2", [128, 256], BF16)
                PTR = ps("PTR", BF16, 1024)
                PGM = ps("PGM")
                PW = Ring("PW", [ps(f"PW{i}") for i in range(1)])
                T.dma("sp", "misc", [
                    (DMA(WSTf[:], self.wsT_h), (), ["WSTf"]),
                    (DMA(BSB[:], self.gmlpb_h.partition_broadcast(128).rearrange("q (g p) -> q g p", g=8)), (), ["BSB"]),
                    (DMA(MNC[:], self.mnorm_h), (), ["MNC"]),
                ])
                T.op("dve", CP(WST[:], WSTf[:]), ["WSTf"], ["WST"])
                wx_uses = [(self.wout_b[m], [("wout_b", m)]) for t in range(cfg.NXT) for m in range(NC)]
                WXS = SlabStream(T, "WX", WX, ["wx0", "wx1", "wx2"], wx_uses)
                uwx = 0

            for h in range(4):
                T.op("dve", MSET(C[h][:], 0.0), (), [("C", h)])
                T.op("pool", MSET(Cb[h][:], 0.0), (), [("Cb", h)])
            T.dma("sp", "misc", [(DMA(KC[:], self.k_c), [("qk_s", "c", 0)], ["KC"])])

            def load_qkt(t):
                b = t % 2
                T.dma("sp", f"qk{b}", [(DMA(QKT[b][:], self.qk_x[:, :, t * TT:(t + 1) * TT]),
                                         [("qk_s", "x", tt) for tt in range(cfg.NXT)], [("QKT", b)])])

            xtiles = list(range(cfg.NXT)) if d == 0 else list(range(cfg.NXT - 1, -1, -1))
            load_qkt(xtiles[0])
            for i, cg in enumerate(order):
                isx = cg >= NCHC
                xc = cg - NCHC
                t, blk = (xc // 4, xc % 4) if isx else (None, cg)
                b = t % 2 if isx else 0
                first_of_tile = isx and blk == (0 if d == 0 else 3)
                last_of_tile = isx and blk == (3 if d == 0 else 0)
                if first_of_tile:
                    k = xtiles.index(t)
                    if k + 1 < len(xtiles):
                        load_qkt(xtiles[k + 1])
                vl, vlk, vk = VL.next()
                T.dma("sp", f"vl{vk}", [(DMA(vl[:].rearrange("p h e -> p (h e)"), self.vext_s[cg]),
                                          [("vext_s", cg)], [vlk])])
                if isx and d == 0:
                    hfs, hfsk, hk = HFS.next()
                if isx and d == 1:
                    hfs, hfsk, hk = HFS.next()
                    T.dma("sp", f"hf{hk}", [(DMA(hfs[:].rearrange("p h e -> p (h e)"), self.hfwd_s[xc]),
                                              [("hfwd_s", xc)], [hfsk])])
                    hsum, hsumk, _ = HSUM.next()
                tk = slice(blk * 128, (blk + 1) * 128)
                for h in range(4):
                    if isx:
                        kT, qT, kkey = QKT[b][:, 4 + h, tk], QKT[b][:, h, tk], ("QKT", b)
                    else:
                        kT, qT, kkey = KC[:, h, tk], None, "KC"
                    wcol = WCOL[:, d, cg, h:h + 1]
                    pk, pkk, _ = PK.next()
                    T.mmgroup([(TR(pk[:, 0:128], kT, IDB[:]), [kkey, "IDB"])], [pkk])
                    kw, kwk, _ = KW.next()
                    T.op("act", ACT(kw[:], pk[:, 0:128], AF.Copy, scale=wcol), [pkk, "WCOL"], [kwk])
                    if isx:
                        pst, pstk, _ = PST.next()
                        T.mmgroup([(MM(pst[:, 0:128], kT, qT, True, True), [kkey])], [pstk])
                        ptt, pttk, _ = PTT.next()
                        T.op("dve", STT(ptt[:], pst[:, 0:128], wcol, MASK[d][:], ALU.mult, ALU.mult),
                             [pstk, "WCOL", ("MASK", d)], [pttk])
                        qd, qdk, _ = QD.next()
                        T.op("act", ACT(qd[:], qT, AF.Copy, scale=DECQ[:, d, cg, h:h + 1]), [kkey, "DECQ"], [qdk])
                        po, pok, _ = PO.next()
                        T.mmgroup([(MM(po[:, 0:257], qd[:], Cb[h][:], True, False), [qdk, ("Cb", h)]),
                                   (MM(po[:, 0:257], ptt[:], vl[:, h, :], False, True), [pttk, vlk])], [pok])
                        dn, dnk, _ = DN.next()
                        T.op("act", ACT(dn[:, 0:1], po[:, 256:257], AF.Abs), [pok], [(dnk, 0)])
                        T.op("dve", TS(dn[:, 1:2], dn[:, 0:1], THR[:, d, cg, h:h + 1], ALU.max),
                             [(dnk, 0), "THR"], [(dnk, 1)])
                        T.op("dve", RCP(dn[:, 2:3], dn[:, 1:2]), [(dnk, 1)], [(dnk, 2)])
                        if d == 0:
                            T.op("act", ACT(hfs[:, h, :], po[:, 0:256], AF.Copy, scale=dn[:, 2:3]),
                                 [pok, (dnk, 2)], [hfsk])
                        else:
                            T.op("dve", STT(hsum[:, h, :], po[:, 0:256], dn[:, 2:3], hfs[:, h, :], ALU.mult, ALU.add),
                                 [pok, (dnk, 2), hfsk], [(hsumk, h)])
                    if i + 1 < NCH:
                        pu, puk, _ = PU.next()
                        T.mmgroup([(MM(pu[:, 0:257], kw[:], vl[:, h, :], True, True), [kwk, vlk])], [puk])
                        T.op("dve", STT(C[h][:], C[h][:], DEC[:, d, cg, h:h + 1], pu[:, 0:257], ALU.mult, ALU.add),
                             [("C", h), "DEC", puk], [("C", h)])
                        T.op("pool", CP(Cb[h][:], C[h][:]), [("C", h)], [("Cb", h)])
                if isx and d == 0:
                    T.dma("sp", f"hfs{hk}", [(DMA(self.hfwd_s[xc], hfs[:].rearrange("p h e -> p (h e)")),
                                               [hfsk], [("hfwd_s", xc)])])
                if isx and d == 1:
                    mb = t % 2
                    tcols = slice(blk * 128, (blk + 1) * 128)
                    ss, ssk, _ = SS4.next()
                    for h in range(4):
                        T.op("act", lambda e, h=h, hsum=hsum, ss=ss: e.activation(
                            out=SQJ2[:], in_=hsum[:, h, :], func=AF.Square, accum_out=ss[:, h:h + 1]),
                            [(hsumk, h)], ["SQJ2", (ssk, h)])
                    T.op("act", ACT(ss[:, 4:8], ss[:, 0:4], AF.Sqrt, bias=cfg.EPS, scale=1.0 / 256),
                         [(ssk, h) for h in range(4)], [(ssk, "r")])
                    T.op("dve", RCP(ss[:, 8:12], ss[:, 4:8]), [(ssk, "r")], [(ssk, "i")])
                    hn, hnk, _ = HN.next()
                    for h in range(4):
                        T.op("act", ACT(hn[:, h * 256:(h + 1) * 256], hsum[:, h, :], AF.Copy, scale=ss[:, 8 + h:9 + h]),
                             [(hsumk, h), (ssk, "i")], [(hnk, h)])
                    sgo, sgok, sk = SGO.next()
                    T.dma("sp", f"sg{sk}", [(DMA(sgo[:], self.oT_s[:, :, xc * 128:(xc + 1) * 128]),
                                              [("oT_s", tt) for tt in range(cfg.NXT)], [sgok])])
                    utl, utlk, uk = UTL.next()
                    T.dma("sp", f"ut{uk}", [(DMA(utl[:], self.uT_s[:, :, xc * 128:(xc + 1) * 128]),
                                              [("uT_s", tt) for tt in range(cfg.NXT)], [utlk])])
                    vgl, vglk, gk = VGL.next()
                    T.dma("sp", f"vgl{gk}", [(DMA(vgl[:], self.vg_s[xc]), [("vg_s", xc)], [vglk])])
                    for fc in range(8):
                        T.mmgroup([(TR(PTR[:, 0:128], hn[:, fc * 128:(fc + 1) * 128], IDB[:]), [(hnk, fc // 2), "IDB"])],
                                  ["PTR"])
                        T.op("dve", STT(MIXT[mb][:, fc, tcols], PTR[:, 0:128], MNC[:, fc:fc + 1], sgo[:, fc, :],
                                        ALU.mult, ALU.mult), ["PTR", "MNC", sgok], [("MIXT", mb, blk)])
                    for g in range(8):
                        T.mmgroup([(MM(PGM[:, 0:128], vgl[:, g * 128:(g + 1) * 128], WST[:, g, :], True, True),
                                    [vglk, "WST"])], ["PGM"])
                        tg, tgk, _ = TG.next()
                        T.op("dve", TT_(tg[:], PGM[:, 0:128], BSB[:, g, :], ALU.add), ["PGM", "BSB"], [tgk])
                        T.op("pool", TT_(MIXT[mb][:, 8 + g, tcols], tg[:], utl[:, g, :], ALU.mult),
                             [tgk, utlk], [("MIXT", mb, blk)])
                    if last_of_tile:
                        off = t * TT
                        for m in range(NC):
                            W, wk = WXS.get(uwx)
                            uwx += 1
                            x1l, x1k, xk = X1L.next()
                            src = self.x1T[:, off:off + TT].rearrange("(c p) n -> p c n", p=128)[:, m, :]
                            T.dma("sp", f"x1l{xk}", [(DMA(x1l[:], src), [("x1T", t)], [x1k])])
                            pw, pwk, _ = PW.next()
                            steps = [(MM(pw[:, :], W[:, k, :], MIXT[mb][:, k, :], k == 0, k == NC - 1),
                                      [wk] + [("MIXT", mb, bb) for bb in range(4)]) for k in range(NC)]
                            T.mmgroup(steps, [pwk])
                            x2s, x2k, sk2 = X2S.next()
                            T.op("dve", STT(x2s[:], pw[:, :], col("G2x", m), x1l[:], ALU.mult, ALU.add),
                                 [pwk, x1k, ("COLS", "G2x")], [x2k])
                            dst = self.x2T[:, off:off + TT].rearrange("(c p) n -> p c n", p=128)[:, m, :]
                            T.dma("sp", f"x2s{sk2}", [(DMA(dst, x2s[:]), [x2k], [("x2T", t)])])
            T.barrier()
            self.run_block()


    def scan_both(self):
        cfg, nc, T = self.cfg, self.nc, self.T
        TT = cfg.TT
        NCH, NCHC = self.NCH, self.NCHC
        order = self.chunk_order
        WCOL, THR, DEC, DECQ, MASK, IDB = self.WCOL, self.THR, self.DEC, self.DECQ, self.MASK, self.IDB
        hout = [self.hfwd_s, self.hbwd_s]
        with ExitStack() as s4:
            def sb(name, shape, dt=F32):
                return s4.enter_context(nc.sbuf_tensor(f"{name}_sc", shape, dt))

            def ps(name, dt=F32, n=512):
                return s4.enter_context(nc.psum_tensor(f"{name}_sc", [128, n], dt))

            KC = sb("KC", [128, 4, cfg.CTX], BF16)
            L = []
            for d in range(2):
                ln = {}
                ln["QKT"] = [sb(f"QKT{d}{i}", [128, 8, TT], BF16) for i in range(2)]
                ln["VL"] = Ring(f"VL{d}", [sb(f"VL{d}{i}", [128, 4, 257], BF16) for i in range(3)])
                ln["C"] = [sb(f"C{d}{h}", [128, 257]) for h in range(4)]
                ln["Cb"] = [sb(f"Cb{d}{h}", [128, 257], BF16) for h in range(4)]
                ln["KW"] = Ring(f"KW{d}", [sb(f"KW{d}{i}", [128, 128], BF16) for i in range(2)])
                ln["PTT"] = Ring(f"PTT{d}", [sb(f"PTT{d}{i}", [128, 128], BF16) for i in range(2)])
                ln["QD"] = Ring(f"QD{d}", [sb(f"QD{d}{i}", [128, 128], BF16) for i in range(2)])
                ln["DN"] = Ring(f"DN{d}", [sb(f"DN{d}{i}", [128, 4]) for i in range(4)])
                ln["HFS"] = Ring(f"HFS{d}", [sb(f"HFS{d}{i}", [128, 4, 256]) for i in range(2)])
                ln["PO"] = ps(f"PO{d}"); ln["PU"] = ps(f"PU{d}"); ln["PST"] = ps(f"PST{d}")
                ln["PK"] = ps(f"PK{d}", BF16, 1024)
                ln["xt"] = list(range(cfg.NXT)) if d == 0 else list(range(cfg.NXT - 1, -1, -1))
                L.append(ln)
                for h in range(4):
                    T.op("dve", MSET(ln["C"][h][:], 0.0), (), [("C", d, h)])
                    T.op("pool", MSET(ln["Cb"][h][:], 0.0), (), [("Cb", d, h)])
            T.dma("sp", "misc", [(DMA(KC[:], self.k_c), [("qk_s", "c", 0)], ["KC"])])

            def load_qkt(d, t):
                b = t % 2
                T.dma("sp", f"qk{d}{b}", [(DMA(L[d]["QKT"][b][:], self.qk_x[:, :, t * TT:(t + 1) * TT]),
                                            [("qk_s", "x", tt) for tt in range(cfg.NXT)], [("QKT", d, b)])])

            for d in range(2):
                load_qkt(d, L[d]["xt"][0])

            def step_loads(i):
                st = []
                for d in range(2):
                    ln = L[d]
                    cg = order[d][i]
                    isx = cg >= NCHC
                    xc = cg - NCHC
                    t, blk = (xc // 4, xc % 4) if isx else (None, cg)
                    vl, vlk, vk = ln["VL"].next()
                    T.dma("sp", f"vl{d}{vk}", [(DMA(vl[:].rearrange("p h e -> p (h e)"), self.vext_s[cg]),
                                                 [("vext_s", cg)], [vlk])])
                    st.append(dict(cg=cg, isx=isx, xc=xc, t=t, blk=blk, vl=vl, vlk=vlk))
                return st

            nxt_st = step_loads(0)
            for i in range(NCH):
                st = nxt_st
                if i + 1 < NCH:
                    nxt_st = step_loads(i + 1)
                for d in range(2):
                    sd = st[d]
                    if sd["isx"] and sd["blk"] == (0 if d == 0 else 3):
                        k = L[d]["xt"].index(sd["t"])
                        if k + 1 < len(L[d]["xt"]):
                            load_qkt(d, L[d]["xt"][k + 1])
                    sd["hfs"] = sd["hfsk"] = sd["hk"] = None
                    if sd["isx"]:
                        sd["hfs"], sd["hfsk"], sd["hk"] = L[d]["HFS"].next()
                for h in range(4):
                    lanes = []
                    for d in range(2):
                        ln, sd = L[d], st[d]
                        tk = slice(sd["blk"] * 128, (sd["blk"] + 1) * 128)
                        if sd["isx"]:
                            b = sd["t"] % 2
                            kT, qT, kkey = ln["QKT"][b][:, 4 + h, tk], ln["QKT"][b][:, h, tk], ("QKT", d, b)
                        else:
                            kT, qT, kkey = KC[:, h, tk], None, "KC"
                        T.mmgroup([(TR(ln["PK"][:, 0:128], kT, IDB[:]), [kkey, "IDB"])], [("PK", d)])
                        if sd["isx"]:
                            T.mmgroup([(MM(ln["PST"][:, 0:128], kT, qT, True, True), [kkey])], [("PST", d)])
                        lanes.append((kT, qT, kkey))
                    bufs = []
                    for d in range(2):
                        ln, sd = L[d], st[d]
                        kT, qT, kkey = lanes[d]
                        cg = sd["cg"]
                        wcol = WCOL[:, d, cg, h:h + 1]
                        kw, kwk, _ = ln["KW"].next()
                        T.op("act", ACT(kw[:], ln["PK"][:, 0:128], AF.Copy, scale=wcol), [("PK", d), "WCOL"], [kwk])
                        ptt = pttk = qd = qdk = None
                        if sd["isx"]:
                            ptt, pttk, _ = ln["PTT"].next()
                            T.op("dve", STT(ptt[:], ln["PST"][:, 0:128], wcol, MASK[d][:], ALU.mult, ALU.mult),
                                 [("PST", d), "WCOL", ("MASK", d)], [pttk])
                            qd, qdk, _ = ln["QD"].next()
                            T.op("act", ACT(qd[:], qT, AF.Copy, scale=DECQ[:, d, cg, h:h + 1]), [kkey, "DECQ"], [qdk])
                        bufs.append((kw, kwk, ptt, pttk, qd, qdk))
                    for d in range(2):
                        ln, sd = L[d], st[d]
                        kw, kwk, ptt, pttk, qd, qdk = bufs[d]
                        vl, vlk = sd["vl"], sd["vlk"]
                        if sd["isx"]:
                            T.mmgroup([(MM(ln["PO"][:, 0:257], qd[:], ln["Cb"][h][:], True, False), [qdk, ("Cb", d, h)]),
                                       (MM(ln["PO"][:, 0:257], ptt[:], vl[:, h, :], False, True), [pttk, vlk])],
                                      [("PO", d)])
                        if i + 1 < NCH:
                            T.mmgroup([(MM(ln["PU"][:, 0:257], kw[:], vl[:, h, :], True, True), [kwk, vlk])],
                                      [("PU", d)])
                    for d in range(2):
                        ln, sd = L[d], st[d]
                        cg = sd["cg"]
                        if sd["isx"]:
                            dn, dnk, _ = ln["DN"].next()
                            T.op("act", ACT(dn[:, 0:1], ln["PO"][:, 256:257], AF.Abs), [("PO", d)], [(dnk, 0)])
                            T.op("dve", TS(dn[:, 1:2], dn[:, 0:1], THR[:, d, cg, h:h + 1], ALU.max),
                                 [(dnk, 0), "THR"], [(dnk, 1)])
                            T.op("dve", RCP(dn[:, 2:3], dn[:, 1:2]), [(dnk, 1)], [(dnk, 2)])
                            T.op("act", ACT(sd["hfs"][:, h, :], ln["PO"][:, 0:256], AF.Copy, scale=dn[:, 2:3]),
                                 [("PO", d), (dnk, 2)], [sd["hfsk"]])
                        if i + 1 < NCH:
                            T.op("dve", STT(ln["C"][h][:], ln["C"][h][:], DEC[:, d, cg, h:h + 1], ln["PU"][:, 0:257],
                                            ALU.mult, ALU.add), [("C", d, h), "DEC", ("PU", d)], [("C", d, h)])
                            T.op("pool", CP(ln["Cb"][h][:], ln["C"][h][:]), [("C", d, h)], [("Cb", d, h)])
                for d in range(2):
                    sd = st[d]
                    if sd["isx"]:
                        T.dma("sp", f"hfs{d}{sd['hk']}", [(DMA(hout[d][sd["xc"]], sd["hfs"][:].rearrange("p h e -> p (h e)")),
                                                           [sd["hfsk"]], [("hscan", d, sd["xc"])])])
            T.barrier()
            self.run_block()

    def out_stage(self):
        cfg, nc, T = self.cfg, self.nc, self.T
        NC, TT = cfg.NC, cfg.TT
        IDB = self.IDB
        col = self.col
        with ExitStack() as s5:
            def sb(name, shape, dt=F32):
                return s5.enter_context(nc.sbuf_tensor(f"{name}_o", shape, dt))

            def ps(name, dt=F32, n=512):
                return s5.enter_context(nc.psum_tensor(f"{name}_o", [128, n], dt))

            HF = Ring("HF", [sb(f"HF{i}", [128, 4, 256]) for i in range(3)])
            HB = Ring("HB", [sb(f"HB{i}", [128, 4, 256]) for i in range(3)])
            HSUM = Ring("HSUM", [sb(f"HSUM{i}", [128, 4, 256]) for i in range(2)])
            HN = Ring("HN", [sb(f"HN{i}", [128, 1024], BF16) for i in range(2)])
            SGO = Ring("SGO", [sb(f"SGO{i}", [128, 8, 128], BF16) for i in range(3)])
            UTL = Ring("UTL", [sb(f"UTL{i}", [128, 8, 128], BF16) for i in range(3)])
            VGL = Ring("VGL", [sb(f"VGL{i}", [128, 1024], BF16) for i in range(3)])
            WSTf = sb("WSTf", [128, 8, 128]); WST = sb("WST", [128, 8, 128], BF16)
            BSB = sb("BSB", [128, 8, 128]); MNC = sb("MNC", [128, 8])
            MIXT = [sb(f"MIXT{i}", [128, 16, TT], BF16) for i in range(2)]
            WX = [sb(f"WX{i}", [128, NC, 128], BF16) for i in range(3)]
            X1L = Ring("X1L", [sb(f"X1L{i}", [128, TT]) for i in range(3)])
            X2S = Ring("X2S", [sb(f"X2S{i}", [128, TT]) for i in range(3)])
            TG = Ring("TG", [sb(f"TG{i}", [128, 128]) for i in range(3)])
            SS4 = Ring("SS4", [sb(f"SS4{i}", [128, 12]) for i in range(2)])
            SQJ2 = sb("SQJ2", [128, 256], BF16)
            PTR = Ring("PTR", [ps(f"PTR{i}", BF16, 1024) for i in range(2)])
            PGM = Ring("PGM", [ps(f"PGM{i}") for i in range(2)])
            PW = Ring("PW", [ps(f"PW{i}") for i in range(2)])
            T.dma("sp", "misc", [
                (DMA(WSTf[:], self.wsT_h), (), ["WSTf"]),
                (DMA(BSB[:], self.gmlpb_h.partition_broadcast(128).rearrange("q (g p) -> q g p", g=8)), (), ["BSB"]),
                (DMA(MNC[:], self.mnorm_h), (), ["MNC"]),
            ])
            T.op("dve", CP(WST[:], WSTf[:]), ["WSTf"], ["WST"])
            wx_uses = [(self.wout_b[m], [("wout_b", m)]) for t in range(cfg.NXT) for m in range(NC)]
            WXS = SlabStream(T, "WX", WX, ["wx0", "wx1", "wx2"], wx_uses)
            uwx = 0
            nxt_all = list(range(cfg.NXT))

            def loads(xc):
                hf, hfk, k = HF.next()
                T.dma("sp", f"hf{k}", [(DMA(hf[:].rearrange("p h e -> p (h e)"), self.hfwd_s[xc]),
                                         [("hscan", 0, xc)], [hfk])])
                hb, hbk, k = HB.next()
                T.dma("sp", f"hb{k}", [(DMA(hb[:].rearrange("p h e -> p (h e)"), self.hbwd_s[xc]),
                                         [("hscan", 1, xc)], [hbk])])
                sgo, sgok, k = SGO.next()
                T.dma("sp", f"sg{k}", [(DMA(sgo[:], self.oT_s[:, :, xc * 128:(xc + 1) * 128]),
                                         [("oT_s", tt) for tt in nxt_all], [sgok])])
                utl, utlk, k = UTL.next()
                T.dma("sp", f"ut{k}", [(DMA(utl[:], self.uT_s[:, :, xc * 128:(xc + 1) * 128]),
                                         [("uT_s", tt) for tt in nxt_all], [utlk])])
                vgl, vglk, k = VGL.next()
                T.dma("sp", f"vgl{k}", [(DMA(vgl[:], self.vg_s[xc]), [("vg_s", xc)], [vglk])])
                return (hf, hfk, hb, hbk, sgo, sgok, utl, utlk, vgl, vglk)

            pending = []

            def wout_group(t, m):
                mb = t % 2
                off = t * TT
                W, wk = WXS.get(t * NC + m)
                x1l, x1k, xk = X1L.next()
                src = self.x1T[:, off:off + TT].rearrange("(c p) n -> p c n", p=128)[:, m, :]
                T.dma("sp", f"x1l{xk}", [(DMA(x1l[:], src), [("x1T", t)], [x1k])])
                pw, pwk, _ = PW.next()
                steps = [(MM(pw[:, :], W[:, k, :], MIXT[mb][:, k, :], k == 0, k == NC - 1),
                          [wk] + [("MIXT", mb, bb) for bb in range(4)]) for k in range(NC)]
                T.mmgroup(steps, [pwk])
                x2s, x2k, sk2 = X2S.next()
                T.op("dve", STT(x2s[:], pw[:, :], col("G2x", m), x1l[:], ALU.mult, ALU.add),
                     [pwk, x1k, ("COLS", "G2x")], [x2k])
                dst = self.x2T[:, off:off + TT].rearrange("(c p) n -> p c n", p=128)[:, m, :]
                T.dma("sp", f"x2s{sk2}", [(DMA(dst, x2s[:]), [x2k], [("x2T", t)])])

            pend = loads(0)
            for xc in range(self.NCHX):
                cur = pend
                if xc + 1 < self.NCHX:
                    pend = loads(xc + 1)
                hf, hfk, hb, hbk, sgo, sgok, utl, utlk, vgl, vglk = cur
                t, blk = xc // 4, xc % 4
                mb = t % 2
                tcols = slice(blk * 128, (blk + 1) * 128)
                hsum, hsumk, _ = HSUM.next()
                T.op("pool", TT_(hsum[:], hf[:], hb[:], ALU.add), [hfk, hbk], [hsumk])
                ss, ssk, _ = SS4.next()
                for h in range(4):
                    T.op("act", lambda e, h=h, hsum=hsum, ss=ss: e.activation(
                        out=SQJ2[:], in_=hsum[:, h, :], func=AF.Square, accum_out=ss[:, h:h + 1]),
                        [hsumk], ["SQJ2", (ssk, h)])
                T.op("act", ACT(ss[:, 4:8], ss[:, 0:4], AF.Sqrt, bias=cfg.EPS, scale=1.0 / 256),
                     [(ssk, h) for h in range(4)], [(ssk, "r")])
                T.op("dve", RCP(ss[:, 8:12], ss[:, 4:8]), [(ssk, "r")], [(ssk, "i")])
                hn, hnk, _ = HN.next()
                for h in range(4):
                    T.op("act", ACT(hn[:, h * 256:(h + 1) * 256], hsum[:, h, :], AF.Copy, scale=ss[:, 8 + h:9 + h]),
                         [hsumk, (ssk, "i")], [(hnk, h)])
                for fc in range(8):
                    ptr, ptrk, _ = PTR.next()
                    T.mmgroup([(TR(ptr[:, 0:128], hn[:, fc * 128:(fc + 1) * 128], IDB[:]), [(hnk, fc // 2), "IDB"])],
                              [ptrk])
                    T.op("dve", STT(MIXT[mb][:, fc, tcols], ptr[:, 0:128], MNC[:, fc:fc + 1], sgo[:, fc, :],
                                    ALU.mult, ALU.mult), [ptrk, "MNC", sgok], [("MIXT", mb, blk)])
                for g in range(8):
                    pgm, pgmk, _ = PGM.next()
                    T.mmgroup([(MM(pgm[:, 0:128], vgl[:, g * 128:(g + 1) * 128], WST[:, g, :], True, True),
                                [vglk, "WST"])], [pgmk])
                    tg, tgk, _ = TG.next()
                    T.op("dve", TT_(tg[:], pgm[:, 0:128], BSB[:, g, :], ALU.add), [pgmk, "BSB"], [tgk])
                    T.op("pool", TT_(MIXT[mb][:, 8 + g, tcols], tg[:], utl[:, g, :], ALU.mult),
                         [tgk, utlk], [("MIXT", mb, blk)])
                if blk == 3:
                    pending.extend((t, m) for m in range(NC))
                else:
                    for _ in range(min(6, len(pending))):
                        wout_group(*pending.pop(0))
            while pending:
                wout_group(*pending.pop(0))
            T.barrier()
            self.run_block()

    def stage0(self):
        cfg, nc, T = self.cfg, self.nc, self.T
        NC = cfg.NC
        with ExitStack() as s0:
            WA = [s0.enter_context(nc.sbuf_tensor(f"WA{i}", [128, NC, 512], F32)) for i in range(2)]
            SC = s0.enter_context(nc.sbuf_tensor("SC", [128, NC, 2], F32))
            CCt = s0.enter_context(nc.sbuf_tensor("CCt", [128, NC, 2], F32))
            BA = s0.enter_context(nc.sbuf_tensor("BA", [128, 9 * NC], F32))
            NRM = s0.enter_context(nc.sbuf_tensor("NRM", [128, 4, NC], F32))
            MODX = s0.enter_context(nc.sbuf_tensor("MODX", [128, 9 * NC], F32))
            MODC = s0.enter_context(nc.sbuf_tensor("MODC", [128, 9 * NC], F32))
            PA = [s0.enter_context(nc.psum_tensor(f"PA{i}", [128, 512], F32)) for i in range(2)]
            PTc = s0.enter_context(nc.psum_tensor("PTc", [128, 512], F32))
            ROWS = s0.enter_context(nc.sbuf_tensor("ROWS", [2, 9 * cfg.D], F32))
            ID2 = s0.enter_context(nc.sbuf_tensor("ID2", [2, 2], F32))

            T.dma("sp", "misc", [
                (DMA(CCt[:], self.cc), (), ["CCt"]),
                (DMA(BA[:], self.b_ada_h), (), ["BA"]),
                (DMA(NRM[:], self.nrm_h), (), ["NRM"]),
            ])
            T.op("act", ACT(SC[:], CCt[:], AF.Silu), ["CCt"], ["SC"])
            T.op("dve", MSET(self.ONES[:], 1.0), (), ["ONES"])
            T.op("pool", MSET(ID2[:], 1.0), (), ["ID2"])
            for sgn in (1, -1):
                T.op("pool", lambda e, sgn=sgn: e.affine_select(
                    out=ID2[:], in_=ID2[:], pattern=[[sgn, 2]], compare_op=ALU.is_ge, fill=0.0, base=0,
                    channel_multiplier=-sgn), ["ID2"], ["ID2"])
            for s in range(cfg.NADA):
                b = s % 2
                T.dma("sp", f"ada{b}", [(DMA(WA[b][:], self.w_ada_h[s]), (), [("WA", b)])])
                steps = [(MM(PA[b][0:2, :], SC[:, c, :], WA[b][:, c, :], c == 0, c == NC - 1), [("WA", b), "SC"])
                         for c in range(NC)]
                T.mmgroup(steps, [("PA", b)])
                T.op("act" if s % 2 == 0 else "dve", CP(ROWS[0:2, s * 512:(s + 1) * 512], PA[b][0:2, :])
                     if s % 2 else ACT(ROWS[0:2, s * 512:(s + 1) * 512], PA[b][0:2, :], AF.Copy),
                     [("PA", b)], [("ROWS", s)])
                if s == max(0, cfg.NADA - 8):
                    gate = [("WA", b)]
                    self.convert("w1out_b", self.w1out_h, self.w1out_b, NC, 4, gate)
                    if cfg.stages >= 2:
                        self.convert("win_fm_b", self.win_fm_h, self.win_fm_b, 12, 4, gate)
                        self.convert("win_tm_b", self.win_tm_h, self.win_tm_b, 4, 2, gate)
                        self.convert("wg_b", self.wg_h, self.wg_b, 1, 1, gate)
            steps = [(TR(PTc[:, 2 * g:2 * g + 2], ROWS[0:2, g * 128:(g + 1) * 128], ID2[:]), [("ROWS", g // 4), "ID2"])
                     for g in range(9 * NC)]
            T.mmgroup(steps, ["PTc"])
            ptv = PTc[:, 0:2 * 9 * NC].rearrange("p (g t) -> p g t", t=2)
            T.op("dve", TT_(MODX[:], ptv[:, :, 0], BA[:], ALU.add), ["PTc", "BA"], ["MODX"])
            T.op("dve", TT_(MODC[:], ptv[:, :, 1], BA[:], ALU.add), ["PTc", "BA"], ["MODC"])
            allmx = ["MODX"]
            allmc = ["MODC"]

            def blk(M, i):
                return M[:, i * NC:(i + 1) * NC]

            def mk_A(name, M, deps, iscale, inorm):
                T.op("dve", STT(self.colblk(name), blk(M, iscale), 1.0, NRM[:, inorm, :], ALU.add, ALU.mult),
                     deps + ["NRM"], [("COLS", name)])

            def mk_copy(name, M, deps, i, mul=1.0):
                T.op("dve", TS(self.colblk(name), blk(M, i), mul, ALU.mult), deps, [("COLS", name)])

            mk_A("A1x", MODX, allmx, 1, 0); mk_copy("SH1x", MODX, allmx, 0); mk_copy("HG1x", MODX, allmx, 2, 0.5)
            mk_A("A2x", MODX, allmx, 4, 1); mk_copy("SH2x", MODX, allmx, 3); mk_copy("G2x", MODX, allmx, 5)
            mk_A("A3x", MODX, allmx, 7, 2); mk_copy("SH3x", MODX, allmx, 6); mk_copy("HG3x", MODX, allmx, 8, 0.5)
            mk_A("A1c", MODC, allmc, 1, 0); mk_copy("SH1c", MODC, allmc, 0); mk_copy("HG1c", MODC, allmc, 2, 0.5)
            mk_A("A2c", MODC, allmc, 4, 1); mk_copy("SH2c", MODC, allmc, 3)
            T.op("dve", CP(self.colblk("FN"), NRM[:, 3, :]), ["NRM"], [("COLS", "FN")])
            if cfg.debug:
                T.dma("sp", "misc", [(DMA(self.cols_out, self.COLS[:]), [("COLS", n) for n in COLNAMES], ())])
            T.barrier()
            self.run_block()

    def ffn_stage(self, mode):
        cfg, nc, T = self.cfg, self.nc, self.T
        D, NC, NK, TT = cfg.D, cfg.NC, cfg.NK, cfg.TT
        col = self.col
        ONES = self.ONES
        if mode == "ffn1":
            tiles = cfg.tiles
            win_b, wout_b, win_name, wout_name = self.w1in_b, self.w1out_b, "w1in_b", "w1out_b"
        else:
            tiles = [t for t in cfg.tiles if t[0] == "x"]
            win_b, wout_b, win_name, wout_name = self.w2in_b, self.w2out_b, "w2in_b", "w2out_b"

        def src_of(tile):
            kind, idx, off, nt = tile
            if mode == "ffn1":
                return (self.xT if kind == "x" else self.ctxT)[:, off:off + nt]
            return self.x2T[:, off:off + nt]

        def colset(kind):
            if mode == "ffn1":
                return ("A1x", "SH1x", "HG1x") if kind == "x" else ("A1c", "SH1c", "HG1c")
            return ("A3x", "SH3x", "HG3x")

        with ExitStack() as s1:
            def sb(name, shape, dt):
                return s1.enter_context(nc.sbuf_tensor(f"{name}_{mode}", shape, dt))

            def ps(name):
                return s1.enter_context(nc.psum_tensor(f"{name}_{mode}", [128, 512], F32))

            Xb = [sb(f"X{i}", [128, NC, TT], F32) for i in range(2)]
            XN = sb("XN", [128, NC, TT], BF16)
            G = sb("G", [128, NK, TT], BF16)
            WI = [sb(f"WI{i}", [128, NC, 2, 128], BF16) for i in range(3)]
            WO = [sb(f"WO{i}", [128, NK, 128], BF16) for i in range(2)]
            SQ = [sb(f"SQ{i}", [128, TT], F32) for i in range(3)]
            TMP = [sb(f"TMP{i}", [128, TT], F32) for i in range(2)]
            SG = [sb(f"SG{i}", [128, TT], F32) for i in range(2)]
            RS = [sb(f"RS{i}", [128, TT], F32) for i in range(2)]
            RT = [sb(f"RT{i}", [128, TT], F32) for i in range(2)]
            if mode == "ffn1":
                STG = [sb(f"STG{i}", [128, TT], BF16) for i in range(4)]
            else:
                STF = [sb(f"STF{i}", [128, TT], F32) for i in range(2)]
            PG = [ps(f"PG{i}") for i in range(2)]
            PU = [ps(f"PU{i}") for i in range(2)]
            PY = [ps(f"PY{i}") for i in range(2)]
            PS = [ps(f"PS{i}") for i in range(2)]
            cnt = {"sq": 0, "tmp": 0, "sg": 0, "stg": 0, "stf": 0}
            ntl = len(tiles)

            def load_x(ti):
                kind, idx, off, nt = tiles[ti]
                b = ti % 2
                src = src_of(tiles[ti]).rearrange("(c p) n -> p c n", p=128)
                T.dma("sp", f"xl{b}", [(DMA(Xb[b][:, :, :nt], src), [("x2T", idx)] if mode == "ffn2" else (),
                                         [("X", b, c) for c in range(NC)])])

            def rstd_from(p, nt):
                T.op("act", ACT(RT[p][:, :nt], PS[p][:, :nt], AF.Sqrt, bias=cfg.EPS, scale=1.0 / D),
                     [("PS", p)], [("RT", p)])
                T.op("dve", RCP(RS[p][:, :nt], RT[p][:, :nt]), [("RT", p)], [("RS", p)])

            def stat_step(p, b, c, nt, first, last):
                q = cnt["sq"] % 3
                cnt["sq"] += 1
                T.op("act", ACT(SQ[q][:, :nt], Xb[b][:, c, :nt], AF.Square), [("X", b, c)], [("SQ", q)])
                T.mmgroup([(MM(PS[p][:, :nt], ONES[:], SQ[q][:, :nt], first, last), [("SQ", q), "ONES"])],
                          [("PS", p)] if first else [])
                if last:
                    T._stamp((), [("PS", p)], ("pe", T.count["pe"]))

            def modulate(b, c, nt, rs, A, SH, out_ap, out_key):
                q = cnt["tmp"] % 2
                cnt["tmp"] += 1
                T.op("dve", TT_(TMP[q][:, :nt], Xb[b][:, c, :nt], RS[rs][:, :nt], ALU.mult),
                     [("X", b, c), ("RS", rs)], [("TMP", q)])
                T.op("act", ACT(out_ap, TMP[q][:, :nt], AF.Identity, bias=col(SH, c), scale=col(A, c)),
                     [("TMP", q), ("COLS", A), ("COLS", SH)], [out_key])

            def norm_in(ti):
                kind, idx, off, nt = tiles[ti]
                b = ti % 2
                A, SH, HG = colset(kind)
                for c in range(NC):
                    stat_step(0, b, c, nt, c == 0, c == NC - 1)
                rstd_from(0, nt)
                for c in range(NC):
                    modulate(b, c, nt, 0, A, SH, XN[:, c, :nt], ("XN", c))

            n_ui, n_uo = ntl * NK, ntl * NC

            def load_wi(u):
                if u >= n_ui:
                    return
                s = u % NK
                sl = u % 3
                T.dma("sp", f"wi{sl}", [(DMA(WI[sl][:], win_b[s]), [(win_name, s)], [("WI", sl)])])

            def load_wo(u):
                if u >= n_uo:
                    return
                m = u % NC
                sl = u % 2
                T.dma("sp", f"wo{sl}", [(DMA(WO[sl][:], wout_b[m]), [(wout_name, m)], [("WO", sl)])])

            load_x(0)
            if ntl > 1:
                load_x(1)
            for u in range(3):
                load_wi(u)
            for u in range(2):
                load_wo(u)
            norm_in(0)
            for ti in range(ntl):
                kind, idx, off, nt = tiles[ti]
                b = ti % 2
                A, SH, HG = colset(kind)
                for s in range(NK):
                    u = ti * NK + s
                    sl, pb = u % 3, u % 2
                    for half, P, pn in ((0, PG, "PG"), (1, PU, "PU")):
                        steps = [(MM(P[pb][:, :nt], WI[sl][:, c, half, :], XN[:, c, :nt], c == 0, c == NC - 1),
                                  [("WI", sl), ("XN", c)]) for c in range(NC)]
                        T.mmgroup(steps, [(pn, pb)])
                    load_wi(u + 3)
                    if s == min(4, NK - 1) and ti >= 1 and ti + 1 < ntl:
                        load_x(ti + 1)
                    q = cnt["sg"] % 2
                    cnt["sg"] += 1
                    T.op("act", ACT(SG[q][:, :nt], PG[pb][:, :nt], AF.Silu), [("PG", pb)], [("SG", q)])
                    T.op("dve", TT_(G[:, s, :nt], SG[q][:, :nt], PU[pb][:, :nt], ALU.mult),
                         [("SG", q), ("PU", pb)], [("G", s)])
                if ti + 1 < ntl:
                    norm_in(ti + 1)
                for m in range(NC):
                    u = ti * NC + m
                    sl, pb = u % 2, u % 2
                    steps = [(MM(PY[pb][:, :nt], WO[sl][:, k, :], G[:, k, :nt], k == 0, k == NK - 1),
                              [("WO", sl), ("G", k)]) for k in range(NK)]
                    T.mmgroup(steps, [("PY", pb)])
                    load_wo(u + 2)
                    T.op("dve", STT(Xb[b][:, m, :nt], PY[pb][:, :nt], col(HG, m), Xb[b][:, m, :nt],
                                    ALU.mult, ALU.add), [("PY", pb), ("X", b, m), ("COLS", HG)], [("X", b, m)])
                    stat_step(1, b, m, nt, m == 0, m == NC - 1)
                    if mode == "ffn1" and kind == "x":
                        dst = self.x1T[:, off:off + nt].rearrange("(c p) n -> p c n", p=128)[:, m, :]
                        T.dma("sp", f"xs{b}", [(DMA(dst, Xb[b][:, m, :nt]), [("X", b, m)], [("x1T", idx)])],
                              wait_prev=(m == 0))
                rstd_from(1, nt)
                if mode == "ffn1":
                    A2, SH2 = ("A2x", "SH2x") if kind == "x" else ("A2c", "SH2c")
                    for c in range(NC):
                        g = cnt["stg"] % 4
                        cnt["stg"] += 1
                        modulate(b, c, nt, 1, A2, SH2, STG[g][:, :nt], ("STG", g))
                        T.dma("sp", f"st{g}", [(DMA(self.xn2[ti, :, c, :nt], STG[g][:, :nt]), [("STG", g)],
                                                 [("xn2", ti)])])
                else:
                    for c in range(NC):
                        g = cnt["stf"] % 2
                        cnt["stf"] += 1
                        T.op("dve", STT(STF[g][:, :nt], Xb[b][:, c, :nt], col("FN", c), RS[1][:, :nt],
                                        ALU.mult, ALU.mult), [("X", b, c), ("RS", 1), ("COLS", "FN")],
                             [("STF", g)])
                        dst = self.oT[:, off:off + nt].rearrange("(c p) n -> p c n", p=128)[:, c, :]
                        T.dma("sp", f"st{g}", [(DMA(dst, STF[g][:, :nt]), [("STF", g)], [("oT", idx)])])
            T.barrier()
            self.run_block()


def build_program(cfg):
    return Builder(cfg).build()


def _cols(v, nchunk):
    return np.ascontiguousarray(np.asarray(v, np.float32).reshape(nchunk, 128).T)


def prep_shared(cfg, inp):
    D, NC, NK = cfg.D, cfg.NC, cfg.NK
    f = lambda a: np.asarray(a, np.float32)
    sh = {}
    wa = f(inp["w_ada"])[0]
    sh["w_ada_h"] = np.ascontiguousarray(wa.reshape(NC, 128, cfg.NADA, 512).transpose(2, 1, 0, 3))
    sh["b_ada_h"] = _cols(f(inp["b_ada"])[0], 9 * NC)
    sh["nrm_h"] = np.ascontiguousarray(np.stack([
        _cols(f(inp["norm_ffn1"])[0], NC), _cols(f(inp["norm_mix"])[0], NC),
        _cols(f(inp["norm_ffn2"])[0], NC), _cols(f(inp["final_norm"]), NC)], axis=1))

    def ffn_in(w):
        return np.ascontiguousarray(w.reshape(NC, 128, 2, NK, 128).transpose(3, 1, 0, 2, 4))

    def ffn_out(w):
        return np.ascontiguousarray(w.reshape(NK, 128, NC, 128).transpose(2, 1, 0, 3))

    sh["w1in_h"] = ffn_in(f(inp["w_ffn1_in"])[0])
    sh["w1out_h"] = ffn_out(f(inp["w_ffn1_out"])[0])
    if cfg.stages >= 2:
        win = f(inp["w_in"])[0]

        def colslab(c0, n):
            return win[:, c0:c0 + n].reshape(NC, 128, n).transpose(1, 0, 2)

        fm_starts = [0, 256, 512, 768] + [2048 + 256 * i for i in range(4)] + [3088 + 256 * i for i in range(4)]
        sh["win_fm_h"] = np.ascontiguousarray(np.stack([colslab(c0, 256) for c0 in fm_starts]))
        tm_starts = [1024, 1536, 4112, 4624]
        sh["win_tm_h"] = np.ascontiguousarray(np.stack([colslab(c0, 512) for c0 in tm_starts]))
        sh["wg_h"] = np.ascontiguousarray(colslab(3072, 16)[None])
        cw = f(inp["conv_w"])[0]
        cb = f(inp["conv_b"])[0]
        cwb = np.concatenate([cw, cb[None]], axis=0)
        sh["convw_h"] = np.ascontiguousarray(cwb.reshape(6, 8, 128).transpose(2, 1, 0))
        bi, bf_ = f(inp["b_igate"])[0], f(inp["b_fgate"])[0]
        sh["gbias_h"] = np.ascontiguousarray(np.stack([bi, bf_], axis=1).reshape(16))
        sh["gnorm_h"] = np.ascontiguousarray(f(inp["gmlp_norm"])[0])
    if cfg.stages >= 5:
        sh["wsT_h"] = np.ascontiguousarray(f(inp["gmlp_w"])[0].transpose(2, 0, 1))
        sh["gmlpb_h"] = np.ascontiguousarray(f(inp["gmlp_b"])[0].reshape(1024))
        sh["mnorm_h"] = _cols(f(inp["mlstm_norm"])[0], 8)
        wo = f(inp["w_out"])[0]
        sh["wout_h"] = np.ascontiguousarray(wo.reshape(NC, 128, NC, 128).transpose(2, 1, 0, 3))
    if cfg.stages >= 6:
        sh["w2in_h"] = ffn_in(f(inp["w_ffn2_in"])[0])
        sh["w2out_h"] = ffn_out(f(inp["w_ffn2_out"])[0])
    return sh


def prep_core(cfg, inp, b):
    f = lambda a: np.asarray(a, np.float32)
    NC = cfg.NC
    d = {}
    d["xT"] = np.ascontiguousarray(f(inp["x"])[b].T)
    d["ctxT"] = np.ascontiguousarray(f(inp["ctx"])[b].T)
    d["cc"] = np.ascontiguousarray(np.stack([_cols(f(inp["c"])[b], NC), _cols(f(inp["c_ctx"]), NC)], axis=2))
    return d


def kernel(**inputs):
    cfg = Cfg()
    B = np.asarray(inputs["x"]).shape[0]
    assert B == 8
    nc = build_program(cfg)
    sh = prep_shared(cfg, inputs)
    in_maps = []
    for b in range(B):
        d = dict(sh)
        d.update(prep_core(cfg, inputs, b))
        in_maps.append(d)
    res = run_bass_kernel_spmd(nc, in_maps, core_ids=list(range(B)))
    out = np.stack([np.asarray(res.results[b]["oT"], np.float32).T for b in range(B)], axis=0)
    return np.ascontiguousarray(out)
```

```python
import numpy as np
from contextlib import ExitStack
import concourse.bass as bass
import concourse.mybir as mybir
from concourse.bass_utils import run_bass_kernel_spmd

F32 = mybir.dt.float32
BF16 = mybir.dt.bfloat16
AF = mybir.ActivationFunctionType
ALU = mybir.AluOpType
AX = mybir.AxisListType


class _Key:
    __slots__ = ("w", "r")

    def __init__(self):
        self.w = None
        self.r = {}


class Tracker:
    ENGS = ("pe", "act", "dve", "pool", "sp")

    def __init__(self, sems):
        self.sems = sems
        self.count = {n: 0 for n in sems}
        self.known = {e: {} for e in self.ENGS}
        self.lists = {e: [] for e in self.ENGS}
        self.keys = {}
        self.n_ops = 0
        self.n_waits = 0
        self.alias = {}

    def _k(self, key):
        k = self.keys.get(key)
        if k is None:
            k = self.keys[key] = _Key()
        return k

    def _need(self, reads, writes, need=None):
        if need is None:
            need = {}
        for key in reads:
            k = self.keys.get(key)
            if k is not None and k.w is not None:
                s, v = k.w
                if need.get(s, 0) < v:
                    need[s] = v
        for key in writes:
            k = self.keys.get(key)
            if k is None:
                continue
            if k.w is not None:
                s, v = k.w
                if need.get(s, 0) < v:
                    need[s] = v
            for s, v in k.r.items():
                if need.get(s, 0) < v:
                    need[s] = v
        return need

    def _waits(self, eng, need):
        kn = self.known[eng]
        for s, v in need.items():
            if kn.get(s, 0) >= v:
                continue
            if s == eng:
                assert v <= self.count[s], f"self-wait on future signal {eng} {v}>{self.count[s]}"
            kn[s] = v
            self.lists[eng].append(("w", self.sems[s], v))
            self.n_waits += 1

    def _stamp(self, reads, writes, stamp):
        s, v = stamp
        for key in reads:
            k = self._k(key)
            if k.r.get(s, 0) < v:
                k.r[s] = v
        for key in writes:
            k = self._k(key)
            k.w = stamp
            k.r = {}

    def op(self, eng, fn, reads=(), writes=()):
        need = self._need(reads, writes)
        self._waits(eng, need)
        self.count[eng] += 1
        self.lists[eng].append(("i", fn, self.sems[eng], 1))
        self._stamp(reads, writes, (eng, self.count[eng]))
        self.n_ops += 1

    def mmgroup(self, steps, writes, eng="pe"):
        n = len(steps)
        for i, (fn, reads) in enumerate(steps):
            need = self._need(reads, writes if i == 0 else ())
            self._waits(eng, need)
            if i == n - 1:
                self.count[eng] += 1
                self.lists[eng].append(("i", fn, self.sems[eng], 1))
            else:
                self.lists[eng].append(("i", fn, None, 0))
            self.n_ops += 1
        stamp = (eng, self.count[eng])
        for fn, reads in steps:
            self._stamp(reads, (), stamp)
        self._stamp((), writes, stamp)

    def new_stage(self):
        self.alias = {}

    def dma(self, eng, sem, dmas, wait_prev=True):
        if not sem.startswith("cv"):
            a = self.alias.get(sem)
            if a is None:
                a = self.alias[sem] = f"g{len(self.alias)}"
                assert a in self.sems, "out of generic DMA semaphores"
            sem = a
        need = {}
        for fn, reads, writes in dmas:
            self._need(reads, writes, need)
        if wait_prev and self.count[sem] > 0 and need.get(sem, 0) < self.count[sem]:
            need[sem] = self.count[sem]
        self._waits(eng, need)
        final = self.count[sem] + 16 * len(dmas)
        for fn, reads, writes in dmas:
            self.lists[eng].append(("i", fn, self.sems[sem], 16))
            self._stamp(reads, writes, (sem, final))
            self.n_ops += 1
        self.count[sem] = final

    def barrier(self):
        need = {s: v for s, v in self.count.items() if v > 0 and not s.startswith("cv")}
        for e in self.ENGS:
            self._waits(e, dict(need))

    def replay(self, eng, bass_eng, start):
        lst = self.lists[eng]
        for item in lst[start:]:
            if item[0] == "w":
                bass_eng.wait_ge(item[1], item[2])
            else:
                ins = item[1](bass_eng)
                if item[2] is not None:
                    ins.then_inc(item[2], item[3])
        return len(lst)


class Cfg:
    def __init__(self, D=2048, DFF=5632, SEQ=4096, CTX=256, TT=512, debug=False, stages=99):
        self.D, self.DFF, self.SEQ, self.CTX, self.TT = D, DFF, SEQ, CTX, TT
        self.NC = D // 128
        self.NK = DFF // 128
        self.NXT = SEQ // TT
        self.debug = debug
        self.stages = stages
        self.tiles = [("c", 0, 0, CTX)] + [("x", t, t * TT, TT) for t in range(self.NXT)]
        self.NADA = 9 * D // 512
        self.EPS = 1e-6


N_CVT_SEMS = 8
N_GEN_SEMS = 28
DMA_SEMS = [f"g{i}" for i in range(N_GEN_SEMS)] + [f"cv{i}" for i in range(N_CVT_SEMS)]


COLNAMES = ["A1x", "SH1x", "HG1x", "A2x", "SH2x", "G2x", "A3x", "SH3x", "HG3x", "A1c", "SH1c", "HG1c",
            "A2c", "SH2c", "FN"]


def MM(o, l, r, st, sp):
    return lambda e: e.matmul(o, lhsT=l, rhs=r, start=st, stop=sp)


def TR(o, i, ident):
    return lambda e: e.transpose(o, i, ident)


def ACT(o, i, func, bias=None, scale=None):
    kw = {}
    if bias is not None:
        kw["bias"] = bias
    if scale is not None:
        kw["scale"] = scale
    return lambda e: e.activation(out=o, in_=i, func=func, **kw)


def TT_(o, a, b, op):
    return lambda e: e.tensor_tensor(out=o, in0=a, in1=b, op=op)


def STT(o, a, s, b, op0, op1):
    return lambda e: e.scalar_tensor_tensor(out=o, in0=a, scalar=s, in1=b, op0=op0, op1=op1)


def TS(o, a, s1, op0, s2=None, op1=None):
    if op1 is None:
        return lambda e: e.tensor_scalar(out=o, in0=a, scalar1=s1, scalar2=None, op0=op0)
    return lambda e: e.tensor_scalar(out=o, in0=a, scalar1=s1, scalar2=s2, op0=op0, op1=op1)


def CP(o, i):
    return lambda e: e.tensor_copy(out=o, in_=i)


def RCP(o, i):
    return lambda e: e.reciprocal(out=o, in_=i)


def MSET(o, v):
    return lambda e: e.memset(o, v)


def DMA(o, i):
    return lambda e: e.dma_start(out=o, in_=i)


class SlabStream:
    def __init__(self, T, name, bufs, sems, uses):
        self.T, self.name, self.bufs, self.sems, self.uses = T, name, bufs, sems, uses
        self.nxt = 0

    def _issue(self, v):
        src, deps = self.uses[v]
        sl = v % len(self.bufs)
        self.T.dma("sp", self.sems[sl], [(DMA(self.bufs[sl][:], src), deps, [(self.name, sl)])])

    def get(self, u):
        d = len(self.bufs)
        while self.nxt < len(self.uses) and self.nxt <= u + d - 1:
            self._issue(self.nxt)
            self.nxt += 1
        sl = u % d
        return self.bufs[sl], (self.name, sl)

    def prefetch(self, u):
        self.get(u)


class Ring:
    def __init__(self, name, bufs):
        self.name, self.bufs, self.i = name, bufs, 0

    def next(self):
        k = self.i % len(self.bufs)
        self.i += 1
        return self.bufs[k], (self.name, k), k


class Builder:
    def __init__(self, cfg):
        self.cfg = cfg
        self.nc = bass.Bass("TRN2", target_bir_lowering=False)
        self.es = ExitStack()

    def din(self, name, shape, dt=F32):
        return self.nc.dram_tensor(name, list(shape), dt, kind="ExternalInput").ap()

    def dscr(self, name, shape, dt):
        return self.nc.dram_tensor(name, list(shape), dt, kind="Internal").ap()

    def dout(self, name, shape, dt=F32):
        return self.nc.dram_tensor(name, list(shape), dt, kind="ExternalOutput").ap()

    def dbg(self, name, shape, dt):
        return self.dout(name, shape, dt) if self.cfg.debug else self.dscr(name, shape, dt)

    def run_block(self):
        T, pos, nc = self.T, self.pos, self.nc
        T.new_stage()
        with nc.Block() as block:
            @block.sync
            def _(e):
                pos["sp"] = T.replay("sp", e, pos["sp"])

            @block.tensor
            def _(e):
                pos["pe"] = T.replay("pe", e, pos["pe"])

            @block.scalar
            def _(e):
                pos["act"] = T.replay("act", e, pos["act"])

            @block.vector
            def _(e):
                pos["dve"] = T.replay("dve", e, pos["dve"])

            @block.gpsimd
            def _(e):
                pos["pool"] = T.replay("pool", e, pos["pool"])

    def col(self, name, j):
        o = COLNAMES.index(name) * self.cfg.NC + j
        return self.COLS[:, o:o + 1]

    def colblk(self, name):
        o = COLNAMES.index(name) * self.cfg.NC
        return self.COLS[:, o:o + self.cfg.NC]

    def convert(self, name, src, dst, nslab, per_dma, extra_reads=()):
        T = self.T
        slab_elems = 1
        for s in src.shape[1:]:
            slab_elems *= s
        assert slab_elems % 2048 == 0
        nd = len(src.shape)
        names = " ".join(f"a{i}" for i in range(1, nd))
        s2 = src.rearrange(f"s {names} -> s ({names})").rearrange("s (r n) -> s r n", n=2048)
        d2 = dst.rearrange(f"s {names} -> s ({names})").rearrange("s (r n) -> s r n", n=2048)
        for g0 in range(0, nslab, per_dma):
            g1 = min(nslab, g0 + per_dma)
            sem = f"cv{self.cvt_i % N_CVT_SEMS}"
            self.cvt_i += 1
            si = s2[g0:g1].rearrange("s r n -> (s r) n")
            do = d2[g0:g1].rearrange("s r n -> (s r) n")
            keys = [(name, s) for s in range(g0, g1)]
            T.dma("pool", sem, [(DMA(do, si), tuple(extra_reads), keys)])

    def build(self):
        cfg, nc, es = self.cfg, self.nc, self.es
        D, NC, NK, TT = cfg.D, cfg.NC, cfg.NK, cfg.TT
        self.xT = self.din("xT", [D, cfg.SEQ])
        self.ctxT = self.din("ctxT", [D, cfg.CTX])
        self.cc = self.din("cc", [128, NC, 2])
        self.w_ada_h = self.din("w_ada_h", [cfg.NADA, 128, NC, 512])
        self.b_ada_h = self.din("b_ada_h", [128, 9 * NC])
        self.nrm_h = self.din("nrm_h", [128, 4, NC])
        self.w1in_h = self.din("w1in_h", [NK, 128, NC, 2, 128])
        self.w1out_h = self.din("w1out_h", [NC, 128, NK, 128])
        self.w1in_b = self.dscr("w1in_b", [NK, 128, NC, 2, 128], BF16)
        self.w1out_b = self.dscr("w1out_b", [NC, 128, NK, 128], BF16)
        NTILES = len(cfg.tiles)
        self.x1T = self.dbg("x1T", [D, cfg.SEQ], F32)
        self.xn2 = self.dbg("xn2", [NTILES, 128, NC, TT], BF16)
        if cfg.debug:
            self.cols_out = self.dout("cols_out", [128, len(COLNAMES) * NC])
        if cfg.stages >= 2:
            self.declare_mix()
        if cfg.stages >= 6:
            self.w2in_h = self.din("w2in_h", [NK, 128, NC, 2, 128])
            self.w2out_h = self.din("w2out_h", [NC, 128, NK, 128])
            self.w2in_b = self.dscr("w2in_b", [NK, 128, NC, 2, 128], BF16)
            self.w2out_b = self.dscr("w2out_b", [NC, 128, NK, 128], BF16)
            self.oT = self.dout("oT", [D, cfg.SEQ])

        with es:
            sems = {}
            for n in list(Tracker.ENGS) + DMA_SEMS:
                sems[n] = es.enter_context(nc.semaphore(n))
            self.T = Tracker(sems)
            self.pos = {e: 0 for e in Tracker.ENGS}
            self.cvt_i = 0
            self.COLS = es.enter_context(nc.sbuf_tensor("COLS", [128, len(COLNAMES) * NC], F32))
            self.ONES = es.enter_context(nc.sbuf_tensor("ONES", [128, 128], F32))

            self.convert("w1in_b", self.w1in_h, self.w1in_b, NK, 8)
            if cfg.stages >= 2:
                self.GT = es.enter_context(nc.sbuf_tensor("GT", [128, self.NCH, 16], F32))
                for nm in ("WCOL", "THR", "DEC", "DECQ"):
                    setattr(self, nm, es.enter_context(nc.sbuf_tensor(nm, [128, 2, self.NCH, 4], F32)))
                self.MASK = [es.enter_context(nc.sbuf_tensor(f"MASK{d}", [128, 128], F32)) for d in range(2)]
                self.IDB = es.enter_context(nc.sbuf_tensor("IDB", [128, 128], BF16))
            self.stage0()
            self.ffn_stage("ffn1")
            if cfg.stages >= 2:
                self.stage2()
            if cfg.stages >= 3:
                self.stage3()
            if cfg.stages >= 4:
                self.scan_both()
            if cfg.stages >= 5:
                self.out_stage()
            if cfg.stages >= 6:
                self.ffn_stage("ffn2")
        return nc


    def declare_mix(self):
        cfg = self.cfg
        D, NC, SEQ, CTX = cfg.D, cfg.NC, cfg.SEQ, cfg.CTX
        self.NCHC, self.NCHX = CTX // 128, SEQ // 128
        self.NCH = self.NCHC + self.NCHX
        self.win_fm_h = self.din("win_fm_h", [12, 128, NC, 256])
        self.win_tm_h = self.din("win_tm_h", [4, 128, NC, 512])
        self.wg_h = self.din("wg_h", [1, 128, NC, 16])
        self.win_fm_b = self.dscr("win_fm_b", [12, 128, NC, 256], BF16)
        self.win_tm_b = self.dscr("win_tm_b", [4, 128, NC, 512], BF16)
        self.wg_b = self.dscr("wg_b", [1, 128, NC, 16], BF16)
        self.convw_h = self.din("convw_h", [128, 8, 6])
        self.gbias_h = self.din("gbias_h", [16])
        self.gnorm_h = self.din("gnorm_h", [1024])
        self.qk_x = self.dbg("qk_x", [128, 8, SEQ], BF16)
        self.k_c = self.dbg("k_c", [128, 4, CTX], BF16)
        self.vext_s = self.dbg("vext_s", [self.NCH, 128, 1028], BF16)
        self.oT_s = self.dbg("oT_s", [128, 8, SEQ], BF16)
        self.uT_s = self.dbg("uT_s", [128, 8, SEQ], BF16)
        self.vg_s = self.dbg("vg_s", [self.NCHX, 128, 1024], BF16)
        self.hfwd_s = self.dbg("hfwd_s", [self.NCHX, 128, 1024], F32)
        self.hbwd_s = self.dbg("hbwd_s", [self.NCHX, 128, 1024], F32)
        if cfg.stages >= 5:
            self.wsT_h = self.din("wsT_h", [128, 8, 128])
            self.gmlpb_h = self.din("gmlpb_h", [1024])
            self.mnorm_h = self.din("mnorm_h", [128, 8])
            self.wout_h = self.din("wout_h", [NC, 128, NC, 128])
            self.wout_b = self.dscr("wout_b", [NC, 128, NC, 128], BF16)
            self.x2T = self.dbg("x2T", [D, SEQ], F32)
        if cfg.debug:
            self.gt_out = self.dout("gt_out", [128, self.NCH, 16])
            self.g3_out = self.dout("g3_out", [128, 4, 2 * self.NCH * 4])

    def stage2(self):
        cfg, nc, T = self.cfg, self.nc, self.T
        NC, TT = cfg.NC, cfg.TT
        tiles = cfg.tiles
        GT = self.GT
        with ExitStack() as s2:
            def sb(name, shape, dt):
                return s2.enter_context(nc.sbuf_tensor(name, shape, dt))

            def ps(name):
                return s2.enter_context(nc.psum_tensor(name, [128, 512], F32))

            XN2 = [sb(f"XN2_{i}", [128, NC, TT], BF16) for i in range(2)]
            FM = [sb(f"FM{i}", [128, NC, 256], BF16) for i in range(3)]
            TM = [sb(f"TM{i}", [128, NC, 512], BF16) for i in range(2)]
            WG = sb("WG", [128, NC, 16], BF16)
            ZQK = sb("ZQK", [128, 8, TT + 6], F32)
            CACC = Ring("CACC", [sb(f"CACC{i}", [128, TT + 2], F32) for i in range(2)])
            QKB = Ring("QKB", [sb(f"QKB{i}", [128, TT + 2], BF16) for i in range(2)])
            STGB = Ring("STGB", [sb(f"STGB{i}", [128, TT], BF16) for i in range(4)])
            VROW = [sb(f"VROW{i}", [128, 4, 257], BF16) for i in range(8)]
            VGF = [sb(f"VGF{i}", [128, 1024], F32) for i in range(4)]
            VGB = Ring("VGB", [sb(f"VGB{i}", [128, 1024], BF16) for i in range(2)])
            SQJ = sb("SQJ", [128, 1024], BF16)
            SSQ = sb("SSQ", [128, 8], F32)
            CW = sb("CW", [128, 8, 6], F32)
            GB16 = sb("GB16", [128, 16], F32)
            GNB = sb("GNB", [128, 1024], F32)
            PF = [ps(f"PF{i}") for i in range(2)]
            PT = [ps(f"PT{i}") for i in range(2)]
            PGt = [ps(f"PGt{i}") for i in range(2)]

            T.dma("sp", "misc", [
                (DMA(CW[:], self.convw_h), (), ["CW"]),
                (DMA(GB16[:], self.gbias_h.partition_broadcast(128)), (), ["GB16"]),
                (DMA(GNB[:], self.gnorm_h.partition_broadcast(128)), (), ["GNB"]),
                (DMA(WG[:], self.wg_b[0]), [("wg_b", 0)], ["WG"]),
            ])
            T.op("dve", MSET(ZQK[:], 0.0), (), [("ZQK", j) for j in range(8)] + ["ZQKc"])
            for i in range(8):
                T.op("pool", MSET(VROW[i][:, :, 256:257], 1.0), (), [("VROW", i)])

            if cfg.stages >= 5:
                self.convert("wout_b", self.wout_h, self.wout_b, NC, 8)
            fm_uses, tm_uses = [], []
            for (kind, idx, off, nt) in tiles:
                for sidx in (range(12) if kind == "x" else (2, 3)):
                    fm_uses.append((self.win_fm_b[sidx], [("win_fm_b", sidx)]))
                for sidx in (range(4) if kind == "x" else (0, 1)):
                    tm_uses.append((self.win_tm_b[sidx], [("win_tm_b", sidx)]))
            FMS = SlabStream(T, "FM", FM, ["fm0", "fm1", "fm2"], fm_uses)
            TMS = SlabStream(T, "TM", TM, ["tm0", "tm1"], tm_uses)

            def load_xn2(ti):
                kind, idx, off, nt = tiles[ti]
                b = ti % 2
                T.dma("sp", f"xn{b}", [(DMA(XN2[b][:, :, :nt], self.xn2[ti, :, :, :nt]), [("xn2", ti)],
                                         [("XN2", b)])])

            load_xn2(0)
            ufm = utm = 0
            nvr = 0
            ngt = 0
            for ti, (kind, idx, off, nt) in enumerate(tiles):
                b = ti % 2
                isx = kind == "x"
                if ti + 1 < len(tiles):
                    load_xn2(ti + 1)
                nblk = nt // 128
                cg0 = off // 128 + (self.NCHC if isx else 0)
                last_x = isx and idx == cfg.NXT - 1
                for sidx in (range(12) if isx else (2, 3)):
                    W, wkey = FMS.get(ufm)
                    ufm += 1
                    for h in range(2):
                        pb = (2 * sidx + h) % 2
                        steps = [(MM(PF[pb][:, :nt], W[:, c, h * 128:(h + 1) * 128], XN2[b][:, c, :nt],
                                     c == 0, c == NC - 1), [wkey, ("XN2", b)]) for c in range(NC)]
                        T.mmgroup(steps, [("PF", pb)])
                        if sidx < 4:
                            j = sidx * 2 + h
                            T.op("dve", CP(ZQK[:, j, 4:4 + nt], PF[pb][:, :nt]), [("PF", pb)], [("ZQK", j)])
                        else:
                            j = ((sidx - 4) % 4) * 2 + h
                            func = AF.Sigmoid if sidx < 8 else AF.Gelu_apprx_tanh
                            dst = (self.oT_s if sidx < 8 else self.uT_s)[:, j, off:off + nt]
                            sg, sgk, k = STGB.next()
                            T.op("act", ACT(sg[:, :nt], PF[pb][:, :nt], func), [("PF", pb)], [sgk])
                            T.dma("sp", f"so{k}", [(DMA(dst, sg[:, :nt]), [sgk],
                                                     [("oT_s" if sidx < 8 else "uT_s", idx)])])
                    if sidx == 3:
                        i0 = 2 if (not isx or idx == 0) else 0
                        i1 = nt + 2 if (not isx or last_x) else nt
                        Wd = i1 - i0
                        for j in (range(8) if isx else range(4, 8)):
                            acc, acck, _ = CACC.next()
                            z = ZQK[:, j, :]
                            T.op("dve", TS(acc[:, :Wd], z[:, i0:i1], CW[:, j, 0:1], ALU.mult, CW[:, j, 5:6], ALU.add),
                                 [("ZQK", j), "ZQKc", "CW"], [acck])
                            for jj in range(1, 5):
                                T.op("dve", STT(acc[:, :Wd], z[:, i0 + jj:i1 + jj], CW[:, j, jj:jj + 1], acc[:, :Wd],
                                                ALU.mult, ALU.add), [("ZQK", j), "ZQKc", "CW", acck], [acck])
                            qb, qbk, k = QKB.next()
                            T.op("act", ACT(qb[:, :Wd], acc[:, :Wd], AF.Silu), [acck], [qbk])
                            if isx:
                                dst = self.qk_x[:, j, off - 2 + i0:off - 2 + i1]
                            else:
                                dst = self.k_c[:, j - 4, i0 - 2:i1 - 2]
                            T.dma("sp", f"sq{k}", [(DMA(dst, qb[:, :Wd]), [qbk], [("qk_s", kind, idx)])])
                        if isx and not last_x:
                            T.op("dve", CP(ZQK[:, :, 0:4], ZQK[:, :, nt:nt + 4]),
                                 [("ZQK", j) for j in range(8)], ["ZQKc"])
                vr_of = {}
                for sidx in (range(4) if isx else (0, 1)):
                    W, wkey = TMS.get(utm)
                    utm += 1
                    for blk in range(nblk):
                        pb = (sidx * nblk + blk) % 2
                        steps = [(MM(PT[pb][:, :], XN2[b][:, c, blk * 128:(blk + 1) * 128], W[:, c, :],
                                     c == 0, c == NC - 1), [wkey, ("XN2", b)]) for c in range(NC)]
                        T.mmgroup(steps, [("PT", pb)])
                        if sidx < 2:
                            if sidx == 0:
                                vr_of[blk] = nvr % 8
                                nvr += 1
                            r = vr_of[blk]
                            T.op("dve", CP(VROW[r][:, 2 * sidx:2 * sidx + 2, 0:256],
                                           PT[pb][:, :].rearrange("p (h e) -> p h e", h=2)),
                                 [("PT", pb)], [("VROW", r)])
                            if sidx == 1:
                                T.dma("sp", f"vr{r}", [(DMA(self.vext_s[cg0 + blk], VROW[r][:].rearrange("p h e -> p (h e)")),
                                                        [("VROW", r)], [("vext_s", cg0 + blk)])])
                        else:
                            hv = sidx - 2
                            T.op("act", ACT(VGF[blk][:, hv * 512:(hv + 1) * 512], PT[pb][:, :], AF.Gelu_apprx_tanh),
                                 [("PT", pb)], [("VGF", blk, hv)])
                            if hv == 1:
                                T.op("act", lambda e, blk=blk: e.activation(
                                    out=SQJ[:], in_=VGF[blk][:], func=AF.Square, accum_out=SSQ[:, blk:blk + 1]),
                                    [("VGF", blk, 0), ("VGF", blk, 1)], ["SQJ", ("SSQ", blk)])
                                T.op("act", ACT(SSQ[:, 4 + blk:5 + blk], SSQ[:, blk:blk + 1], AF.Sqrt,
                                                bias=cfg.EPS, scale=1.0 / 1024), [("SSQ", blk)], [("SSR", blk)])
                                T.op("dve", RCP(SSQ[:, 4 + blk:5 + blk], SSQ[:, 4 + blk:5 + blk]),
                                     [("SSR", blk)], [("SSR", blk)])
                                vb, vbk, k = VGB.next()
                                T.op("dve", STT(vb[:], VGF[blk][:], SSQ[:, 4 + blk:5 + blk], GNB[:], ALU.mult, ALU.mult),
                                     [("VGF", blk, 0), ("VGF", blk, 1), ("SSR", blk), "GNB"], [vbk])
                                T.dma("sp", f"vg{k}", [(DMA(self.vg_s[cg0 - self.NCHC + blk], vb[:]), [vbk],
                                                         [("vg_s", cg0 - self.NCHC + blk)])])
                for blk in range(nblk):
                    pb = ngt % 2
                    ngt += 1
                    steps = [(MM(PGt[pb][:, 0:16], XN2[b][:, c, blk * 128:(blk + 1) * 128], WG[:, c, :],
                                 c == 0, c == NC - 1), ["WG", ("XN2", b)]) for c in range(NC)]
                    T.mmgroup(steps, [("PGt", pb)])
                    T.op("dve", TT_(GT[:, cg0 + blk, :], PGt[pb][:, 0:16], GB16[:], ALU.add),
                         [("PGt", pb), "GB16"], [("GT", cg0 + blk)])
            if cfg.debug:
                T.dma("sp", "misc", [(DMA(self.gt_out, GT[:]), [("GT", c) for c in range(self.NCH)], ())])
            T.barrier()
            self.run_block()


    def stage3(self):
        cfg, nc, T = self.cfg, self.nc, self.T
        NCH, NCHC = self.NCH, self.NCHC
        NQ = NCH * 4
        assert NQ % 2 == 0 and NQ // 2 <= 128 and 2 * NQ <= 512
        HQ = NQ // 2
        GT, ONES = self.GT, self.ONES
        with ExitStack() as s3:
            def sb(name, shape, dt=F32):
                return s3.enter_context(nc.sbuf_tensor(name, shape, dt))

            def ps(name):
                return s3.enter_context(nc.psum_tensor(name, [128, 512], F32))

            LI = sb("LI", [128, 2, NCH, 4]); ZF = sb("ZF", [128, 2, NCH, 4]); AB = sb("AB", [128, 2 * NQ])
            EX = sb("EX", [128, 2 * NQ]); LN = sb("LN", [128, 2 * NQ]); MN = sb("MN", [128, 2 * NQ])
            LF = sb("LF", [128, 2, NQ]); Bs = sb("Bs", [128, 2, NQ]); TOT = sb("TOT", [128, 2, NQ])
            A = sb("A", [128, 2, NQ]); AMX = sb("AMX", [128, 4]); AMR = sb("AMR", [1, 2, NCH, 4])
            MROW = sb("MROW", [1, 2, NCH, 4]); MOUT = sb("MOUT", [1, 2, NCH, 4]); MINR = sb("MINR", [1, 2, NCH, 4])
            ZERO4 = sb("ZERO4", [1, 4]); MBC = sb("MBC", [128, 2 * NQ]); T1 = sb("T1", [128, 2 * NQ])
            T2 = sb("T2", [128, 2 * NQ]); T3 = sb("T3", [128, 2 * NQ])
            TRI = [sb("TRIF", [128, 128]), sb("TRIB", [128, 128])]
            IDF = sb("IDF", [128, 128])
            PB = [ps("PB0"), ps("PB1")]
            PBt = [ps("PBt0"), ps("PBt1")]
            PX = ps("PX"); PR = ps("PR"); PM = ps("PM"); PN = ps("PN")
            GTv = GT[:].rearrange("p c (d f h) -> p c d f h", d=2, f=2, h=4)
            gtk = [("GT", c) for c in range(NCH)]

            for d in range(2):
                T.op("pool", MSET(TRI[d][:], 1.0), (), [("TRI", d)])
                T.op("pool", lambda e, d=d: e.affine_select(
                    out=TRI[d][:], in_=TRI[d][:], pattern=[[1 if d == 0 else -1, 128]], compare_op=ALU.is_ge,
                    fill=0.0, base=0, channel_multiplier=(-1 if d == 0 else 1)), [("TRI", d)], [("TRI", d)])
                T.op("dve", TS(self.MASK[d][:], TRI[d][:], 128.0 ** -0.5, ALU.mult), [("TRI", d)], [("MASK", d)])
            T.op("dve", TT_(IDF[:], TRI[0][:], TRI[1][:], ALU.mult), [("TRI", 0), ("TRI", 1)], ["IDF"])
            T.op("dve", CP(self.IDB[:], IDF[:]), ["IDF"], ["IDB"])

            for d in range(2):
                T.op("dve", CP(LI[:, d], GTv[:, :, d, 0, :]), gtk, ["LI"])
                T.op("dve", CP(ZF[:, d], GTv[:, :, d, 1, :]), gtk, ["ZF"])
            zf = ZF[:].rearrange("p d c h -> p (d c h)")
            T.op("act", ACT(AB[:], zf, AF.Abs), ["ZF"], ["AB"])
            T.op("act", ACT(EX[:], AB[:], AF.Exp, scale=-1.0), ["AB"], ["EX"])
            T.op("act", ACT(LN[:], EX[:], AF.Ln, bias=1.0), ["EX"], ["LN"])
            T.op("dve", TS(MN[:], zf, 0.0, ALU.min), ["ZF"], ["MN"])
            T.op("dve", TT_(LF[:].rearrange("p d q -> p (d q)"), MN[:], LN[:], ALU.subtract), ["MN", "LN"], ["LF"])
            for d in range(2):
                T.mmgroup([(MM(PB[d][:, 0:NQ], TRI[d][:], LF[:, d, :], True, True), [("TRI", d), "LF"])],
                          [("PB", d, 0)])
                T.mmgroup([(MM(PBt[d][:, 0:NQ], ONES[:], LF[:, d, :], True, True), ["ONES", "LF"])],
                          [("PB", d, 1)])
                T.op("dve", CP(Bs[:, d, :], PB[d][:, 0:NQ]), [("PB", d, 0)], ["Bs"])
                T.op("dve", CP(TOT[:, d, :], PBt[d][:, 0:NQ]), [("PB", d, 1)], ["TOT"])
            T.op("dve", TT_(A[:].rearrange("p d q -> p (d q)"), LI[:].rearrange("p d c h -> p (d c h)"),
                            Bs[:].rearrange("p d q -> p (d q)"), ALU.subtract), ["LI", "Bs"], ["A"])
            for q in range(4):
                d, hf = q // 2, q % 2
                T.mmgroup([(TR(PX[0:HQ, q * 128:(q + 1) * 128], A[:, d, hf * HQ:(hf + 1) * HQ], IDF[:]),
                            ["A", "IDF"])], [("PX", q)])
            T.op("dve", lambda e: e.tensor_reduce(out=AMX[0:HQ, :], in_=PX[0:HQ, :].rearrange("p (q n) -> p q n", q=4),
                                                  axis=AX.X, op=ALU.max), [("PX", q) for q in range(4)], ["AMX"])
            for q in range(4):
                T.mmgroup([(TR(PR[0:1, q * HQ:(q + 1) * HQ], AMX[0:HQ, q:q + 1], IDF[0:HQ, 0:HQ]), ["AMX", "IDF"])],
                          [("PR", q)])
            T.op("dve", CP(AMR[:].rearrange("p d c h -> p (d c h)"), PR[0:1, 0:2 * NQ]),
                 [("PR", q) for q in range(4)], ["AMR"])
            T.op("dve", MSET(ZERO4[:], 0.0), (), ["ZERO4"])
            T.op("dve", MSET(MINR[:], 0.0), (), ["MINR"])
            order = [list(range(NCH)), list(range(NCHC - 1, -1, -1)) + list(range(NCH - 1, NCHC - 1, -1))]
            self.chunk_order = order
            cur = [ZERO4[:], ZERO4[:]]
            curk = ["ZERO4", "ZERO4"]
            for i in range(NCH):
                for d in range(2):
                    c = order[d][i]
                    T.op("dve", TT_(MROW[0:1, d, c, :], cur[d], AMR[0:1, d, c, :], ALU.max),
                         [curk[d], "AMR"], [("MROW", d, c)])
                    T.op("dve", TT_(MOUT[0:1, d, c, :], TOT[0:1, d, c * 4:(c + 1) * 4], MROW[0:1, d, c, :], ALU.add),
                         ["TOT", ("MROW", d, c)], [("MOUT", d, c)])
                    if i + 1 < NCH:
                        cn = order[d][i + 1]
                        T.op("pool", CP(MINR[0:1, d, cn, :], MOUT[0:1, d, c, :]), [("MOUT", d, c), "MINR"],
                             [("MINRc", d, cn)])
                    cur[d], curk[d] = MOUT[0:1, d, c, :], ("MOUT", d, c)
            allm = [("MROW", d, c) for d in range(2) for c in range(NCH)]
            allmin = [("MINRc", d, c) for d in range(2) for c in range(NCH)] + ["MINR"]
            T.mmgroup([(MM(PM[:, 0:2 * NQ], ONES[0:1, :], MROW[:].rearrange("p d c h -> p (d c h)"), True, True),
                        allm + ["ONES"])], ["PM"])
            T.mmgroup([(MM(PN[:, 0:2 * NQ], ONES[0:1, :], MINR[:].rearrange("p d c h -> p (d c h)"), True, True),
                        allmin + ["ONES"])], ["PN"])
            T.op("dve", CP(MBC[:], PM[:, 0:2 * NQ]), ["PM"], ["MBC"])
            fl = lambda t: t[:].rearrange("p d c h -> p (d c h)")
            T.op("dve", TT_(T1[:], A[:].rearrange("p d q -> p (d q)"), MBC[:], ALU.subtract), ["A", "MBC"], ["T1"])
            T.op("act", ACT(fl(self.WCOL), T1[:], AF.Exp), ["T1"], ["WCOL"])
            T.op("dve", TT_(T2[:], Bs[:].rearrange("p d q -> p (d q)"), MBC[:], ALU.add), ["Bs", "MBC"], ["T2"])
            T.op("act", ACT(fl(self.THR), T2[:], AF.Exp, scale=-1.0), ["T2"], ["THR"])
            T.op("dve", TT_(T3[:], PN[:, 0:2 * NQ], MBC[:], ALU.subtract), ["PN", "MBC"], ["T3"])
            T.op("act", ACT(fl(self.DEC), T3[:], AF.Exp), ["T3"], ["DEC"])
            T.op("dve", TS(fl(self.DECQ), fl(self.DEC), 128.0 ** -0.5, ALU.mult), ["DEC"], ["DECQ"])
            if cfg.debug:
                T.dma("sp", "misc", [
                    (DMA(self.g3_out[:, 0], fl(self.WCOL)), ["WCOL"], ()),
                    (DMA(self.g3_out[:, 1], fl(self.THR)), ["THR"], ()),
                    (DMA(self.g3_out[:, 2], fl(self.DEC)), ["DEC"], ()),
                    (DMA(self.g3_out[:, 3], fl(self.DECQ)), ["DECQ"], ()),
                ])
            T.barrier()
            self.run_block()


    def scan_stage(self, d):
        cfg, nc, T = self.cfg, self.nc, self.T
        NC, TT = cfg.NC, cfg.TT
        NCH, NCHC, NCHX = self.NCH, self.NCHC, self.NCHX
        order = self.chunk_order[d]
        WCOL, THR, DEC, DECQ, MASK, IDB = self.WCOL, self.THR, self.DEC, self.DECQ, self.MASK, self.IDB
        col = self.col
        nbuf = 2 if d == 0 else 1
        with ExitStack() as s4:
            def sb(name, shape, dt=F32):
                return s4.enter_context(nc.sbuf_tensor(f"{name}_d{d}", shape, dt))

            def ps(name, dt=F32, n=512):
                return s4.enter_context(nc.psum_tensor(f"{name}_d{d}", [128, n], dt))

            QKT = [sb(f"QKT{i}", [128, 8, TT], BF16) for i in range(2)]
            KC = sb("KC", [128, 4, cfg.CTX], BF16)
            VL = Ring("VL", [sb(f"VL{i}", [128, 4, 257], BF16) for i in range(3)])
            C = [sb(f"C{h}", [128, 257]) for h in range(4)]
            Cb = [sb(f"Cb{h}", [128, 257], BF16) for h in range(4)]
            KW = Ring("KW", [sb(f"KW{i}", [128, 128], BF16) for i in range(2)])
            PTT = Ring("PTT", [sb(f"PTT{i}", [128, 128], BF16) for i in range(2)])
            QD = Ring("QD", [sb(f"QD{i}", [128, 128], BF16) for i in range(2)])
            DN = Ring("DN", [sb(f"DN{i}", [128, 4]) for i in range(4)])
            HFS = Ring("HFS", [sb(f"HFS{i}", [128, 4, 256]) for i in range(2)])
            PO = Ring("PO", [ps(f"PO{i}") for i in range(2)])
            PU = Ring("PU", [ps(f"PUs{i}") for i in range(nbuf)])
            PST = Ring("PST", [ps(f"PST{i}") for i in range(nbuf)])
            PK = Ring("PK", [ps(f"PK{i}", BF16, 1024) for i in range(nbuf)])
            if d == 1:
                HSUM = Ring("HSUM", [sb(f"HSUM{i}", [128, 4, 256]) for i in range(2)])
                HN = Ring("HN", [sb(f"HN{i}", [128, 1024], BF16) for i in range(2)])
                SGO = Ring("SGO", [sb(f"SGO{i}", [128, 8, 128], BF16) for i in range(2)])
                UTL = Ring("UTL", [sb(f"UTL{i}", [128, 8, 128], BF16) for i in range(2)])
                VGL = Ring("VGL", [sb(f"VGL{i}", [128, 1024], BF16) for i in range(2)])
                WSTf = sb("WSTf", [128, 8, 128]); WST = sb("WST", [128, 8, 128], BF16)
                BSB = sb("BSB", [128, 8, 128]); MNC = sb("MNC", [128, 8])
                MIXT = [sb(f"MIXT{i}", [128, 16, TT], BF16) for i in range(2)]
                WX = [sb(f"WX{i}", [128, NC, 128], BF16) for i in range(3)]
                X1L = Ring("X1L", [sb(f"X1L{i}", [128, TT]) for i in range(3)])
                X2S = Ring("X2S", [sb(f"X2S{i}", [128, TT]) for i in range(3)])
                TG = Ring("TG", [sb(f"TG{i}", [128, 128]) for i in range(2)])
                SS4 = Ring("SS4", [sb(f"SS4{i}", [128, 12]) for i in range(2)])
                SQJ2 = sb("SQJ2", [128, 256], BF16)
                PTR = ps("PTR", BF16, 1024)
                PGM = ps("PGM")
                PW = Ring("PW", [ps(f"PW{i}") for i in range(1)])
                T.dma("sp", "misc", [
                    (DMA(WSTf[:], self.wsT_h), (), ["WSTf"]),
                    (DMA(BSB[:], self.gmlpb_h.partition_broadcast(128).rearrange("q (g p) -> q g p", g=8)), (), ["BSB"]),
                    (DMA(MNC[:], self.mnorm_h), (), ["MNC"]),
                ])
                T.op("dve", CP(WST[:], WSTf[:]), ["WSTf"], ["WST"])
                wx_uses = [(self.wout_b[m], [("wout_b", m)]) for t in range(cfg.NXT) for m in range(NC)]
                WXS = SlabStream(T, "WX", WX, ["wx0", "wx1", "wx2"], wx_uses)
                uwx = 0

            for h in range(4):
                T.op("dve", MSET(C[h][:], 0.0), (), [("C", h)])
                T.op("pool", MSET(Cb[h][:], 0.0), (), [("Cb", h)])
            T.dma("sp", "misc", [(DMA(KC[:], self.k_c), [("qk_s", "c", 0)], ["KC"])])

            def load_qkt(t):
                b = t % 2
                T.dma("sp", f"qk{b}", [(DMA(QKT[b][:], self.qk_x[:, :, t * TT:(t + 1) * TT]),
                                         [("qk_s", "x", tt) for tt in range(cfg.NXT)], [("QKT", b)])])

            xtiles = list(range(cfg.NXT)) if d == 0 else list(range(cfg.NXT - 1, -1, -1))
            load_qkt(xtiles[0])
            for i, cg in enumerate(order):
                isx = cg >= NCHC
                xc = cg - NCHC
                t, blk = (xc // 4, xc % 4) if isx else (None, cg)
                b = t % 2 if isx else 0
                first_of_tile = isx and blk == (0 if d == 0 else 3)
                last_of_tile = isx and blk == (3 if d == 0 else 0)
                if first_of_tile:
                    k = xtiles.index(t)
                    if k + 1 < len(xtiles):
                        load_qkt(xtiles[k + 1])
                vl, vlk, vk = VL.next()
                T.dma("sp", f"vl{vk}", [(DMA(vl[:].rearrange("p h e -> p (h e)"), self.vext_s[cg]),
                                          [("vext_s", cg)], [vlk])])
                if isx and d == 0:
                    hfs, hfsk, hk = HFS.next()
                if isx and d == 1:
                    hfs, hfsk, hk = HFS.next()
                    T.dma("sp", f"hf{hk}", [(DMA(hfs[:].rearrange("p h e -> p (h e)"), self.hfwd_s[xc]),
                                              [("hfwd_s", xc)], [hfsk])])
                    hsum, hsumk, _ = HSUM.next()
                tk = slice(blk * 128, (blk + 1) * 128)
                for h in range(4):
                    if isx:
                        kT, qT, kkey = QKT[b][:, 4 + h, tk], QKT[b][:, h, tk], ("QKT", b)
                    else:
                        kT, qT, kkey = KC[:, h, tk], None, "KC"
                    wcol = WCOL[:, d, cg, h:h + 1]
                    pk, pkk, _ = PK.next()
                    T.mmgroup([(TR(pk[:, 0:128], kT, IDB[:]), [kkey, "IDB"])], [pkk])
                    kw, kwk, _ = KW.next()
                    T.op("act", ACT(kw[:], pk[:, 0:128], AF.Copy, scale=wcol), [pkk, "WCOL"], [kwk])
                    if isx:
                        pst, pstk, _ = PST.next()
                        T.mmgroup([(MM(pst[:, 0:128], kT, qT, True, True), [kkey])], [pstk])
                        ptt, pttk, _ = PTT.next()
                        T.op("dve", STT(ptt[:], pst[:, 0:128], wcol, MASK[d][:], ALU.mult, ALU.mult),
                             [pstk, "WCOL", ("MASK", d)], [pttk])
                        qd, qdk, _ = QD.next()
                        T.op("act", ACT(qd[:], qT, AF.Copy, scale=DECQ[:, d, cg, h:h + 1]), [kkey, "DECQ"], [qdk])
                        po, pok, _ = PO.next()
                        T.mmgroup([(MM(po[:, 0:257], qd[:], Cb[h][:], True, False), [qdk, ("Cb", h)]),
                                   (MM(po[:, 0:257], ptt[:], vl[:, h, :], False, True), [pttk, vlk])], [pok])
                        dn, dnk, _ = DN.next()
                        T.op("act", ACT(dn[:, 0:1], po[:, 256:257], AF.Abs), [pok], [(dnk, 0)])
                        T.op("dve", TS(dn[:, 1:2], dn[:, 0:1], THR[:, d, cg, h:h + 1], ALU.max),
                             [(dnk, 0), "THR"], [(dnk, 1)])
                        T.op("dve", RCP(dn[:, 2:3], dn[:, 1:2]), [(dnk, 1)], [(dnk, 2)])
                        if d == 0:
                            T.op("act", ACT(hfs[:, h, :], po[:, 0:256], AF.Copy, scale=dn[:, 2:3]),
                                 [pok, (dnk, 2)], [hfsk])
                        else:
                            T.op("dve", STT(hsum[:, h, :], po[:, 0:256], dn[:, 2:3], hfs[:, h, :], ALU.mult, ALU.add),
                                 [pok, (dnk, 2), hfsk], [(hsumk, h)])
                    if i + 1 < NCH:
                        pu, puk, _ = PU.next()
                        T.mmgroup([(MM(pu[:, 0:257], kw[:], vl[:, h, :], True, True), [kwk, vlk])], [puk])
                        T.op("dve", STT(C[h][:], C[h][:], DEC[:, d, cg, h:h + 1], pu[:, 0:257], ALU.mult, ALU.add),
                             [("C", h), "DEC", puk], [("C", h)])
                        T.op("pool", CP(Cb[h][:], C[h][:]), [("C", h)], [("Cb", h)])
                if isx and d == 0:
                    T.dma("sp", f"hfs{hk}", [(DMA(self.hfwd_s[xc], hfs[:].rearrange("p h e -> p (h e)")),
                                               [hfsk], [("hfwd_s", xc)])])
                if isx and d == 1:
                    mb = t % 2
                    tcols = slice(blk * 128, (blk + 1) * 128)
                    ss, ssk, _ = SS4.next()
                    for h in range(4):
                        T.op("act", lambda e, h=h, hsum=hsum, ss=ss: e.activation(
                            out=SQJ2[:], in_=hsum[:, h, :], func=AF.Square, accum_out=ss[:, h:h + 1]),
                            [(hsumk, h)], ["SQJ2", (ssk, h)])
                    T.op("act", ACT(ss[:, 4:8], ss[:, 0:4], AF.Sqrt, bias=cfg.EPS, scale=1.0 / 256),
                         [(ssk, h) for h in range(4)], [(ssk, "r")])
                    T.op("dve", RCP(ss[:, 8:12], ss[:, 4:8]), [(ssk, "r")], [(ssk, "i")])
                    hn, hnk, _ = HN.next()
                    for h in range(4):
                        T.op("act", ACT(hn[:, h * 256:(h + 1) * 256], hsum[:, h, :], AF.Copy, scale=ss[:, 8 + h:9 + h]),
                             [(hsumk, h), (ssk, "i")], [(hnk, h)])
                    sgo, sgok, sk = SGO.next()
                    T.dma("sp", f"sg{sk}", [(DMA(sgo[:], self.oT_s[:, :, xc * 128:(xc + 1) * 128]),
                                              [("oT_s", tt) for tt in range(cfg.NXT)], [sgok])])
                    utl, utlk, uk = UTL.next()
                    T.dma("sp", f"ut{uk}", [(DMA(utl[:], self.uT_s[:, :, xc * 128:(xc + 1) * 128]),
                                              [("uT_s", tt) for tt in range(cfg.NXT)], [utlk])])
                    vgl, vglk, gk = VGL.next()
                    T.dma("sp", f"vgl{gk}", [(DMA(vgl[:], self.vg_s[xc]), [("vg_s", xc)], [vglk])])
                    for fc in range(8):
                        T.mmgroup([(TR(PTR[:, 0:128], hn[:, fc * 128:(fc + 1) * 128], IDB[:]), [(hnk, fc // 2), "IDB"])],
                                  ["PTR"])
                        T.op("dve", STT(MIXT[mb][:, fc, tcols], PTR[:, 0:128], MNC[:, fc:fc + 1], sgo[:, fc, :],
                                        ALU.mult, ALU.mult), ["PTR", "MNC", sgok], [("MIXT", mb, blk)])
                    for g in range(8):
                        T.mmgroup([(MM(PGM[:, 0:128], vgl[:, g * 128:(g + 1) * 128], WST[:, g, :], True, True),
                                    [vglk, "WST"])], ["PGM"])
                        tg, tgk, _ = TG.next()
                        T.op("dve", TT_(tg[:], PGM[:, 0:128], BSB[:, g, :], ALU.add), ["PGM", "BSB"], [tgk])
                        T.op("pool", TT_(MIXT[mb][:, 8 + g, tcols], tg[:], utl[:, g, :], ALU.mult),
                             [tgk, utlk], [("MIXT", mb, blk)])
                    if last_of_tile:
                        off = t * TT
                        for m in range(NC):
                            W, wk = WXS.get(uwx)
                            uwx += 1
                            x1l, x1k, xk = X1L.next()
                            src = self.x1T[:, off:off + TT].rearrange("(c p) n -> p c n", p=128)[:, m, :]
                            T.dma("sp", f"x1l{xk}", [(DMA(x1l[:], src), [("x1T", t)], [x1k])])
                            pw, pwk, _ = PW.next()
                            steps = [(MM(pw[:, :], W[:, k, :], MIXT[mb][:, k, :], k == 0, k == NC - 1),
                                      [wk] + [("MIXT", mb, bb) for bb in range(4)]) for k in range(NC)]
                            T.mmgroup(steps, [pwk])
                            x2s, x2k, sk2 = X2S.next()
                            T.op("dve", STT(x2s[:], pw[:, :], col("G2x", m), x1l[:], ALU.mult, ALU.add),
                                 [pwk, x1k, ("COLS", "G2x")], [x2k])
                            dst = self.x2T[:, off:off + TT].rearrange("(c p) n -> p c n", p=128)[:, m, :]
                            T.dma("sp", f"x2s{sk2}", [(DMA(dst, x2s[:]), [x2k], [("x2T", t)])])
            T.barrier()
            self.run_block()


    def scan_both(self):
        cfg, nc, T = self.cfg, self.nc, self.T
        TT = cfg.TT
        NCH, NCHC = self.NCH, self.NCHC
        order = self.chunk_order
        WCOL, THR, DEC, DECQ, MASK, IDB = self.WCOL, self.THR, self.DEC, self.DECQ, self.MASK, self.IDB
        hout = [self.hfwd_s, self.hbwd_s]
        with ExitStack() as s4:
            def sb(name, shape, dt=F32):
                return s4.enter_context(nc.sbuf_tensor(f"{name}_sc", shape, dt))

            def ps(name, dt=F32, n=512):
                return s4.enter_context(nc.psum_tensor(f"{name}_sc", [128, n], dt))

            KC = sb("KC", [128, 4, cfg.CTX], BF16)
            L = []
            for d in range(2):
                ln = {}
                ln["QKT"] = [sb(f"QKT{d}{i}", [128, 8, TT], BF16) for i in range(2)]
                ln["VL"] = Ring(f"VL{d}", [sb(f"VL{d}{i}", [128, 4, 257], BF16) for i in range(3)])
                ln["C"] = [sb(f"C{d}{h}", [128, 257]) for h in range(4)]
                ln["Cb"] = [sb(f"Cb{d}{h}", [128, 257], BF16) for h in range(4)]
                ln["KW"] = Ring(f"KW{d}", [sb(f"KW{d}{i}", [128, 128], BF16) for i in range(2)])
                ln["PTT"] = Ring(f"PTT{d}", [sb(f"PTT{d}{i}", [128, 128], BF16) for i in range(2)])
                ln["QD"] = Ring(f"QD{d}", [sb(f"QD{d}{i}", [128, 128], BF16) for i in range(2)])
                ln["DN"] = Ring(f"DN{d}", [sb(f"DN{d}{i}", [128, 4]) for i in range(4)])
                ln["HFS"] = Ring(f"HFS{d}", [sb(f"HFS{d}{i}", [128, 4, 256]) for i in range(2)])
                ln["PO"] = ps(f"PO{d}"); ln["PU"] = ps(f"PU{d}"); ln["PST"] = ps(f"PST{d}")
                ln["PK"] = ps(f"PK{d}", BF16, 1024)
                ln["xt"] = list(range(cfg.NXT)) if d == 0 else list(range(cfg.NXT - 1, -1, -1))
                L.append(ln)
                for h in range(4):
                    T.op("dve", MSET(ln["C"][h][:], 0.0), (), [("C", d, h)])
                    T.op("pool", MSET(ln["Cb"][h][:], 0.0), (), [("Cb", d, h)])
            T.dma("sp", "misc", [(DMA(KC[:], self.k_c), [("qk_s", "c", 0)], ["KC"])])
            if cfg.stages >= 6:
                self.convert("w2in_b", self.w2in_h, self.w2in_b, cfg.NK, 8)
                self.convert("w2out_b", self.w2out_h, self.w2out_b, cfg.NC, 4)

            def load_qkt(d, t):
                b = t % 2
                T.dma("sp", f"qk{d}{b}", [(DMA(L[d]["QKT"][b][:], self.qk_x[:, :, t * TT:(t + 1) * TT]),
                                            [("qk_s", "x", tt) for tt in range(cfg.NXT)], [("QKT", d, b)])])

            for d in range(2):
                load_qkt(d, L[d]["xt"][0])

            def step_loads(i):
                st = []
                for d in range(2):
                    ln = L[d]
                    cg = order[d][i]
                    isx = cg >= NCHC
                    xc = cg - NCHC
                    t, blk = (xc // 4, xc % 4) if isx else (None, cg)
                    vl, vlk, vk = ln["VL"].next()
                    T.dma("sp", f"vl{d}{vk}", [(DMA(vl[:].rearrange("p h e -> p (h e)"), self.vext_s[cg]),
                                                 [("vext_s", cg)], [vlk])])
                    st.append(dict(cg=cg, isx=isx, xc=xc, t=t, blk=blk, vl=vl, vlk=vlk))
                return st

            nxt_st = step_loads(0)
            for i in range(NCH):
                st = nxt_st
                if i + 1 < NCH:
                    nxt_st = step_loads(i + 1)
                for d in range(2):
                    sd = st[d]
                    if sd["isx"] and sd["blk"] == (0 if d == 0 else 3):
                        k = L[d]["xt"].index(sd["t"])
                        if k + 1 < len(L[d]["xt"]):
                            load_qkt(d, L[d]["xt"][k + 1])
                    sd["hfs"] = sd["hfsk"] = sd["hk"] = None
                    if sd["isx"]:
                        sd["hfs"], sd["hfsk"], sd["hk"] = L[d]["HFS"].next()
                for h in range(4):
                    lanes = []
                    for d in range(2):
                        ln, sd = L[d], st[d]
                        tk = slice(sd["blk"] * 128, (sd["blk"] + 1) * 128)
                        if sd["isx"]:
                            b = sd["t"] % 2
                            kT, qT, kkey = ln["QKT"][b][:, 4 + h, tk], ln["QKT"][b][:, h, tk], ("QKT", d, b)
                        else:
                            kT, qT, kkey = KC[:, h, tk], None, "KC"
                        T.mmgroup([(TR(ln["PK"][:, 0:128], kT, IDB[:]), [kkey, "IDB"])], [("PK", d)])
                        if sd["isx"]:
                            T.mmgroup([(MM(ln["PST"][:, 0:128], kT, qT, True, True), [kkey])], [("PST", d)])
                        lanes.append((kT, qT, kkey))
                    bufs = []
                    for d in range(2):
                        ln, sd = L[d], st[d]
                        kT, qT, kkey = lanes[d]
                        cg = sd["cg"]
                        wcol = WCOL[:, d, cg, h:h + 1]
                        kw, kwk, _ = ln["KW"].next()
                        T.op("act", ACT(kw[:], ln["PK"][:, 0:128], AF.Copy, scale=wcol), [("PK", d), "WCOL"], [kwk])
                        ptt = pttk = qd = qdk = None
                        if sd["isx"]:
                            ptt, pttk, _ = ln["PTT"].next()
                            T.op("dve", STT(ptt[:], ln["PST"][:, 0:128], wcol, MASK[d][:], ALU.mult, ALU.mult),
                                 [("PST", d), "WCOL", ("MASK", d)], [pttk])
                            qd, qdk, _ = ln["QD"].next()
                            T.op("act", ACT(qd[:], qT, AF.Copy, scale=DECQ[:, d, cg, h:h + 1]), [kkey, "DECQ"], [qdk])
                        bufs.append((kw, kwk, ptt, pttk, qd, qdk))
                    for d in range(2):
                        ln, sd = L[d], st[d]
                        kw, kwk, ptt, pttk, qd, qdk = bufs[d]
                        vl, vlk = sd["vl"], sd["vlk"]
                        if sd["isx"]:
                            T.mmgroup([(MM(ln["PO"][:, 0:257], qd[:], ln["Cb"][h][:], True, False), [qdk, ("Cb", d, h)]),
                                       (MM(ln["PO"][:, 0:257], ptt[:], vl[:, h, :], False, True), [pttk, vlk])],
                                      [("PO", d)])
                        if i + 1 < NCH:
                            T.mmgroup([(MM(ln["PU"][:, 0:257], kw[:], vl[:, h, :], True, True), [kwk, vlk])],
                                      [("PU", d)])
                    for d in range(2):
                        ln, sd = L[d], st[d]
                        cg = sd["cg"]
                        if sd["isx"]:
                            dn, dnk, _ = ln["DN"].next()
                            T.op("act", ACT(dn[:, 0:1], ln["PO"][:, 256:257], AF.Abs), [("PO", d)], [(dnk, 0)])
                            T.op("dve", TS(dn[:, 1:2], dn[:, 0:1], THR[:, d, cg, h:h + 1], ALU.max),
                                 [(dnk, 0), "THR"], [(dnk, 1)])
                            T.op("dve", RCP(dn[:, 2:3], dn[:, 1:2]), [(dnk, 1)], [(dnk, 2)])
                            T.op("act", ACT(sd["hfs"][:, h, :], ln["PO"][:, 0:256], AF.Copy, scale=dn[:, 2:3]),
                                 [("PO", d), (dnk, 2)], [sd["hfsk"]])
                        if i + 1 < NCH:
                            T.op("dve", STT(ln["C"][h][:], ln["C"][h][:], DEC[:, d, cg, h:h + 1], ln["PU"][:, 0:257],
                                            ALU.mult, ALU.add), [("C", d, h), "DEC", ("PU", d)], [("C", d, h)])
                            T.op("dve", CP(ln["Cb"][h][:], ln["C"][h][:]), [("C", d, h)], [("Cb", d, h)])
                for d in range(2):
                    sd = st[d]
                    if sd["isx"]:
                        T.dma("sp", f"hfs{d}{sd['hk']}", [(DMA(hout[d][sd["xc"]], sd["hfs"][:].rearrange("p h e -> p (h e)")),
                                                           [sd["hfsk"]], [("hscan", d, sd["xc"])])])
            T.barrier()
            self.run_block()

    def out_stage(self):
        cfg, nc, T = self.cfg, self.nc, self.T
        NC, TT = cfg.NC, cfg.TT
        IDB = self.IDB
        col = self.col
        with ExitStack() as s5:
            def sb(name, shape, dt=F32):
                return s5.enter_context(nc.sbuf_tensor(f"{name}_o", shape, dt))

            def ps(name, dt=F32, n=512):
                return s5.enter_context(nc.psum_tensor(f"{name}_o", [128, n], dt))

            HF = Ring("HF", [sb(f"HF{i}", [128, 4, 256]) for i in range(3)])
            HB = Ring("HB", [sb(f"HB{i}", [128, 4, 256]) for i in range(3)])
            HSUM = Ring("HSUM", [sb(f"HSUM{i}", [128, 4, 256]) for i in range(2)])
            HN = Ring("HN", [sb(f"HN{i}", [128, 1024], BF16) for i in range(2)])
            SGO = [sb(f"SGO{i}", [128, 8, TT], BF16) for i in range(2)]
            UTL = [sb(f"UTL{i}", [128, 8, TT], BF16) for i in range(2)]
            VGL = Ring("VGL", [sb(f"VGL{i}", [128, 1024], BF16) for i in range(3)])
            WSTf = sb("WSTf", [128, 8, 128]); WST = sb("WST", [128, 8, 128], BF16)
            BSB = sb("BSB", [128, 8, 128]); MNC = sb("MNC", [128, 8])
            MIXT = [sb(f"MIXT{i}", [128, 16, TT], BF16) for i in range(2)]
            WX = [sb(f"WX{i}", [128, NC, 128], BF16) for i in range(3)]
            X1L = Ring("X1L", [sb(f"X1L{i}", [128, TT]) for i in range(3)])
            X2S = Ring("X2S", [sb(f"X2S{i}", [128, TT]) for i in range(3)])
            TG = Ring("TG", [sb(f"TG{i}", [128, 128]) for i in range(3)])
            SS4 = Ring("SS4", [sb(f"SS4{i}", [128, 12]) for i in range(2)])
            SQJ2 = sb("SQJ2", [128, 256], BF16)
            PTR = Ring("PTR", [ps(f"PTR{i}", BF16, 1024) for i in range(2)])
            PGM = Ring("PGM", [ps(f"PGM{i}") for i in range(2)])
            PW = Ring("PW", [ps(f"PW{i}") for i in range(2)])
            T.dma("sp", "misc", [
                (DMA(WSTf[:], self.wsT_h), (), ["WSTf"]),
                (DMA(BSB[:], self.gmlpb_h.partition_broadcast(128).rearrange("q (g p) -> q g p", g=8)), (), ["BSB"]),
                (DMA(MNC[:], self.mnorm_h), (), ["MNC"]),
            ])
            T.op("dve", CP(WST[:], WSTf[:]), ["WSTf"], ["WST"])
            wx_uses = [(self.wout_b[m], [("wout_b", m)]) for t in range(cfg.NXT) for m in range(NC)]
            WXS = SlabStream(T, "WX", WX, ["wx0", "wx1", "wx2"], wx_uses)
            uwx = 0
            nxt_all = list(range(cfg.NXT))

            def loads(xc):
                hf, hfk, k = HF.next()
                T.dma("sp", f"hf{k}", [(DMA(hf[:].rearrange("p h e -> p (h e)"), self.hfwd_s[xc]),
                                         [("hscan", 0, xc)], [hfk])])
                hb, hbk, k = HB.next()
                T.dma("sp", f"hb{k}", [(DMA(hb[:].rearrange("p h e -> p (h e)"), self.hbwd_s[xc]),
                                         [("hscan", 1, xc)], [hbk])])
                t_, blk_ = xc // 4, xc % 4
                k = t_ % 2
                if blk_ == 0:
                    T.dma("sp", f"sg{k}", [(DMA(SGO[k][:], self.oT_s[:, :, t_ * TT:(t_ + 1) * TT]),
                                             [("oT_s", tt) for tt in nxt_all], [("SGO", k)])])
                    T.dma("sp", f"ut{k}", [(DMA(UTL[k][:], self.uT_s[:, :, t_ * TT:(t_ + 1) * TT]),
                                             [("uT_s", tt) for tt in nxt_all], [("UTL", k)])])
                cs = slice(blk_ * 128, (blk_ + 1) * 128)
                sgo, sgok, utl, utlk = SGO[k][:, :, cs], ("SGO", k), UTL[k][:, :, cs], ("UTL", k)
                vgl, vglk, k = VGL.next()
                T.dma("sp", f"vgl{k}", [(DMA(vgl[:], self.vg_s[xc]), [("vg_s", xc)], [vglk])])
                return (hf, hfk, hb, hbk, sgo, sgok, utl, utlk, vgl, vglk)

            pending = []

            def wout_group(t, m):
                mb = t % 2
                off = t * TT
                W, wk = WXS.get(t * NC + m)
                x1l, x1k, xk = X1L.next()
                src = self.x1T[:, off:off + TT].rearrange("(c p) n -> p c n", p=128)[:, m, :]
                T.dma("sp", f"x1l{xk}", [(DMA(x1l[:], src), [("x1T", t)], [x1k])])
                pw, pwk, _ = PW.next()
                steps = [(MM(pw[:, :], W[:, k, :], MIXT[mb][:, k, :], k == 0, k == NC - 1),
                          [wk] + [("MIXT", mb, bb) for bb in range(4)]) for k in range(NC)]
                T.mmgroup(steps, [pwk])
                x2s, x2k, sk2 = X2S.next()
                T.op("dve", STT(x2s[:], pw[:, :], col("G2x", m), x1l[:], ALU.mult, ALU.add),
                     [pwk, x1k, ("COLS", "G2x")], [x2k])
                dst = self.x2T[:, off:off + TT].rearrange("(c p) n -> p c n", p=128)[:, m, :]
                T.dma("sp", f"x2s{sk2}", [(DMA(dst, x2s[:]), [x2k], [("x2T", t)])])

            pend = loads(0)
            for xc in range(self.NCHX):
                cur = pend
                if xc + 1 < self.NCHX:
                    pend = loads(xc + 1)
                hf, hfk, hb, hbk, sgo, sgok, utl, utlk, vgl, vglk = cur
                t, blk = xc // 4, xc % 4
                mb = t % 2
                tcols = slice(blk * 128, (blk + 1) * 128)
                hsum, hsumk, _ = HSUM.next()
                T.op("pool", TT_(hsum[:], hf[:], hb[:], ALU.add), [hfk, hbk], [hsumk])
                ss, ssk, _ = SS4.next()
                for h in range(4):
                    T.op("act", lambda e, h=h, hsum=hsum, ss=ss: e.activation(
                        out=SQJ2[:], in_=hsum[:, h, :], func=AF.Square, accum_out=ss[:, h:h + 1]),
                        [hsumk], ["SQJ2", (ssk, h)])
                T.op("act", ACT(ss[:, 4:8], ss[:, 0:4], AF.Sqrt, bias=cfg.EPS, scale=1.0 / 256),
                     [(ssk, h) for h in range(4)], [(ssk, "r")])
                T.op("dve", RCP(ss[:, 8:12], ss[:, 4:8]), [(ssk, "r")], [(ssk, "i")])
                hn, hnk, _ = HN.next()
                for h in range(4):
                    T.op("act", ACT(hn[:, h * 256:(h + 1) * 256], hsum[:, h, :], AF.Copy, scale=ss[:, 8 + h:9 + h]),
                         [hsumk, (ssk, "i")], [(hnk, h)])
                for fc in range(8):
                    ptr, ptrk, _ = PTR.next()
                    T.mmgroup([(TR(ptr[:, 0:128], hn[:, fc * 128:(fc + 1) * 128], IDB[:]), [(hnk, fc // 2), "IDB"])],
                              [ptrk])
                    T.op("dve", STT(MIXT[mb][:, fc, tcols], ptr[:, 0:128], MNC[:, fc:fc + 1], sgo[:, fc, :],
                                    ALU.mult, ALU.mult), [ptrk, "MNC", sgok], [("MIXT", mb, blk)])
                for g in range(8):
                    pgm, pgmk, _ = PGM.next()
                    T.mmgroup([(MM(pgm[:, 0:128], vgl[:, g * 128:(g + 1) * 128], WST[:, g, :], True, True),
                                [vglk, "WST"])], [pgmk])
                    tg, tgk, _ = TG.next()
                    T.op("dve", TT_(tg[:], pgm[:, 0:128], BSB[:, g, :], ALU.add), [pgmk, "BSB"], [tgk])
                    T.op("pool", TT_(MIXT[mb][:, 8 + g, tcols], tg[:], utl[:, g, :], ALU.mult),
                         [tgk, utlk], [("MIXT", mb, blk)])
                if blk == 3:
                    pending.extend((t, m) for m in range(NC))
                else:
                    for _ in range(min(6, len(pending))):
                        wout_group(*pending.pop(0))
            while pending:
                wout_group(*pending.pop(0))
            T.barrier()
            self.run_block()

    def stage0(self):
        cfg, nc, T = self.cfg, self.nc, self.T
        NC = cfg.NC
        with ExitStack() as s0:
            WA = [s0.enter_context(nc.sbuf_tensor(f"WA{i}", [128, NC, 512], F32)) for i in range(2)]
            SC = s0.enter_context(nc.sbuf_tensor("SC", [128, NC, 2], F32))
            CCt = s0.enter_context(nc.sbuf_tensor("CCt", [128, NC, 2], F32))
            BA = s0.enter_context(nc.sbuf_tensor("BA", [128, 9 * NC], F32))
            NRM = s0.enter_context(nc.sbuf_tensor("NRM", [128, 4, NC], F32))
            MODX = s0.enter_context(nc.sbuf_tensor("MODX", [128, 9 * NC], F32))
            MODC = s0.enter_context(nc.sbuf_tensor("MODC", [128, 9 * NC], F32))
            PA = [s0.enter_context(nc.psum_tensor(f"PA{i}", [128, 512], F32)) for i in range(2)]
            PTc = s0.enter_context(nc.psum_tensor("PTc", [128, 512], F32))
            ROWS = s0.enter_context(nc.sbuf_tensor("ROWS", [2, 9 * cfg.D], F32))
            ID2 = s0.enter_context(nc.sbuf_tensor("ID2", [2, 2], F32))

            T.dma("sp", "misc", [
                (DMA(CCt[:], self.cc), (), ["CCt"]),
                (DMA(BA[:], self.b_ada_h), (), ["BA"]),
                (DMA(NRM[:], self.nrm_h), (), ["NRM"]),
            ])
            T.op("act", ACT(SC[:], CCt[:], AF.Silu), ["CCt"], ["SC"])
            T.op("dve", MSET(self.ONES[:], 1.0), (), ["ONES"])
            T.op("pool", MSET(ID2[:], 1.0), (), ["ID2"])
            for sgn in (1, -1):
                T.op("pool", lambda e, sgn=sgn: e.affine_select(
                    out=ID2[:], in_=ID2[:], pattern=[[sgn, 2]], compare_op=ALU.is_ge, fill=0.0, base=0,
                    channel_multiplier=-sgn), ["ID2"], ["ID2"])
            for s in range(cfg.NADA):
                b = s % 2
                T.dma("sp", f"ada{b}", [(DMA(WA[b][:], self.w_ada_h[s]), (), [("WA", b)])])
                steps = [(MM(PA[b][0:2, :], SC[:, c, :], WA[b][:, c, :], c == 0, c == NC - 1), [("WA", b), "SC"])
                         for c in range(NC)]
                T.mmgroup(steps, [("PA", b)])
                T.op("act" if s % 2 == 0 else "dve", CP(ROWS[0:2, s * 512:(s + 1) * 512], PA[b][0:2, :])
                     if s % 2 else ACT(ROWS[0:2, s * 512:(s + 1) * 512], PA[b][0:2, :], AF.Copy),
                     [("PA", b)], [("ROWS", s)])
                if s == max(0, cfg.NADA - 8):
                    gate = [("WA", b)]
                    self.convert("w1out_b", self.w1out_h, self.w1out_b, NC, 4, gate)
                    if cfg.stages >= 2:
                        self.convert("win_fm_b", self.win_fm_h, self.win_fm_b, 12, 4, gate)
                        self.convert("win_tm_b", self.win_tm_h, self.win_tm_b, 4, 2, gate)
                        self.convert("wg_b", self.wg_h, self.wg_b, 1, 1, gate)
            steps = [(TR(PTc[:, 2 * g:2 * g + 2], ROWS[0:2, g * 128:(g + 1) * 128], ID2[:]), [("ROWS", g // 4), "ID2"])
                     for g in range(9 * NC)]
            T.mmgroup(steps, ["PTc"])
            ptv = PTc[:, 0:2 * 9 * NC].rearrange("p (g t) -> p g t", t=2)
            T.op("dve", TT_(MODX[:], ptv[:, :, 0], BA[:], ALU.add), ["PTc", "BA"], ["MODX"])
            T.op("dve", TT_(MODC[:], ptv[:, :, 1], BA[:], ALU.add), ["PTc", "BA"], ["MODC"])
            allmx = ["MODX"]
            allmc = ["MODC"]

            def blk(M, i):
                return M[:, i * NC:(i + 1) * NC]

            def mk_A(name, M, deps, iscale, inorm):
                T.op("dve", STT(self.colblk(name), blk(M, iscale), 1.0, NRM[:, inorm, :], ALU.add, ALU.mult),
                     deps + ["NRM"], [("COLS", name)])

            def mk_copy(name, M, deps, i, mul=1.0):
                T.op("dve", TS(self.colblk(name), blk(M, i), mul, ALU.mult), deps, [("COLS", name)])

            mk_A("A1x", MODX, allmx, 1, 0); mk_copy("SH1x", MODX, allmx, 0); mk_copy("HG1x", MODX, allmx, 2, 0.5)
            mk_A("A2x", MODX, allmx, 4, 1); mk_copy("SH2x", MODX, allmx, 3); mk_copy("G2x", MODX, allmx, 5)
            mk_A("A3x", MODX, allmx, 7, 2); mk_copy("SH3x", MODX, allmx, 6); mk_copy("HG3x", MODX, allmx, 8, 0.5)
            mk_A("A1c", MODC, allmc, 1, 0); mk_copy("SH1c", MODC, allmc, 0); mk_copy("HG1c", MODC, allmc, 2, 0.5)
            mk_A("A2c", MODC, allmc, 4, 1); mk_copy("SH2c", MODC, allmc, 3)
            T.op("dve", CP(self.colblk("FN"), NRM[:, 3, :]), ["NRM"], [("COLS", "FN")])
            if cfg.debug:
                T.dma("sp", "misc", [(DMA(self.cols_out, self.COLS[:]), [("COLS", n) for n in COLNAMES], ())])
            T.barrier()
            self.run_block()

    def ffn_stage(self, mode):
        cfg, nc, T = self.cfg, self.nc, self.T
        D, NC, NK, TT = cfg.D, cfg.NC, cfg.NK, cfg.TT
        col = self.col
        ONES = self.ONES
        if mode == "ffn1":
            tiles = cfg.tiles
            win_b, wout_b, win_name, wout_name = self.w1in_b, self.w1out_b, "w1in_b", "w1out_b"
        else:
            tiles = [t for t in cfg.tiles if t[0] == "x"]
            win_b, wout_b, win_name, wout_name = self.w2in_b, self.w2out_b, "w2in_b", "w2out_b"

        def src_of(tile):
            kind, idx, off, nt = tile
            if mode == "ffn1":
                return (self.xT if kind == "x" else self.ctxT)[:, off:off + nt]
            return self.x2T[:, off:off + nt]

        def colset(kind):
            if mode == "ffn1":
                return ("A1x", "SH1x", "HG1x") if kind == "x" else ("A1c", "SH1c", "HG1c")
            return ("A3x", "SH3x", "HG3x")

        with ExitStack() as s1:
            def sb(name, shape, dt):
                return s1.enter_context(nc.sbuf_tensor(f"{name}_{mode}", shape, dt))

            def ps(name):
                return s1.enter_context(nc.psum_tensor(f"{name}_{mode}", [128, 512], F32))

            Xb = [sb(f"X{i}", [128, NC, TT], F32) for i in range(2)]
            XN = sb("XN", [128, NC, TT], BF16)
            G = sb("G", [128, NK, TT], BF16)
            WI = [sb(f"WI{i}", [128, NC, 2, 128], BF16) for i in range(3)]
            WO = [sb(f"WO{i}", [128, NK, 128], BF16) for i in range(2)]
            SQ = [sb(f"SQ{i}", [128, TT], F32) for i in range(3)]
            TMP = [sb(f"TMP{i}", [128, TT], F32) for i in range(2)]
            SG = [sb(f"SG{i}", [128, TT], F32) for i in range(2)]
            RS = [sb(f"RS{i}", [128, TT], F32) for i in range(2)]
            RT = [sb(f"RT{i}", [128, TT], F32) for i in range(2)]
            if mode == "ffn1":
                STG = [sb(f"STG{i}", [128, TT], BF16) for i in range(4)]
            else:
                STF = [sb(f"STF{i}", [128, TT], F32) for i in range(2)]
            PG = [ps(f"PG{i}") for i in range(2)]
            PU = [ps(f"PU{i}") for i in range(2)]
            PY = [ps(f"PY{i}") for i in range(2)]
            PS = [ps(f"PS{i}") for i in range(2)]
            cnt = {"sq": 0, "tmp": 0, "sg": 0, "stg": 0, "stf": 0}
            ntl = len(tiles)

            def load_x(ti):
                kind, idx, off, nt = tiles[ti]
                b = ti % 2
                src = src_of(tiles[ti]).rearrange("(c p) n -> p c n", p=128)
                T.dma("sp", f"xl{b}", [(DMA(Xb[b][:, :, :nt], src), [("x2T", idx)] if mode == "ffn2" else (),
                                         [("X", b, c) for c in range(NC)])])

            def rstd_from(p, nt):
                T.op("act", ACT(RT[p][:, :nt], PS[p][:, :nt], AF.Sqrt, bias=cfg.EPS, scale=1.0 / D),
                     [("PS", p)], [("RT", p)])
                T.op("dve", RCP(RS[p][:, :nt], RT[p][:, :nt]), [("RT", p)], [("RS", p)])

            def stat_step(p, b, c, nt, first, last):
                q = cnt["sq"] % 3
                cnt["sq"] += 1
                T.op("act", ACT(SQ[q][:, :nt], Xb[b][:, c, :nt], AF.Square), [("X", b, c)], [("SQ", q)])
                T.mmgroup([(MM(PS[p][:, :nt], ONES[:], SQ[q][:, :nt], first, last), [("SQ", q), "ONES"])],
                          [("PS", p)] if first else [])
                if last:
                    T._stamp((), [("PS", p)], ("pe", T.count["pe"]))

            def modulate(b, c, nt, rs, A, SH, out_ap, out_key):
                q = cnt["tmp"] % 2
                cnt["tmp"] += 1
                T.op("dve", TT_(TMP[q][:, :nt], Xb[b][:, c, :nt], RS[rs][:, :nt], ALU.mult),
                     [("X", b, c), ("RS", rs)], [("TMP", q)])
                T.op("act", ACT(out_ap, TMP[q][:, :nt], AF.Identity, bias=col(SH, c), scale=col(A, c)),
                     [("TMP", q), ("COLS", A), ("COLS", SH)], [out_key])

            def norm_in(ti):
                kind, idx, off, nt = tiles[ti]
                b = ti % 2
                A, SH, HG = colset(kind)
                for c in range(NC):
                    stat_step(0, b, c, nt, c == 0, c == NC - 1)
                rstd_from(0, nt)
                for c in range(NC):
                    modulate(b, c, nt, 0, A, SH, XN[:, c, :nt], ("XN", c))

            n_ui, n_uo = ntl * NK, ntl * NC

            def load_wi(u):
                if u >= n_ui:
                    return
                s = u % NK
                sl = u % 3
                T.dma("sp", f"wi{sl}", [(DMA(WI[sl][:], win_b[s]), [(win_name, s)], [("WI", sl)])])

            def load_wo(u):
                if u >= n_uo:
                    return
                m = u % NC
                sl = u % 2
                T.dma("sp", f"wo{sl}", [(DMA(WO[sl][:], wout_b[m]), [(wout_name, m)], [("WO", sl)])])

            load_x(0)
            if ntl > 1:
                load_x(1)
            for u in range(3):
                load_wi(u)
            for u in range(2):
                load_wo(u)
            norm_in(0)
            for ti in range(ntl):
                kind, idx, off, nt = tiles[ti]
                b = ti % 2
                A, SH, HG = colset(kind)
                for s in range(NK):
                    u = ti * NK + s
                    sl, pb = u % 3, u % 2
                    for half, P, pn in ((0, PG, "PG"), (1, PU, "PU")):
                        steps = [(MM(P[pb][:, :nt], WI[sl][:, c, half, :], XN[:, c, :nt], c == 0, c == NC - 1),
                                  [("WI", sl), ("XN", c)]) for c in range(NC)]
                        T.mmgroup(steps, [(pn, pb)])
                    load_wi(u + 3)
                    if s == min(4, NK - 1) and ti >= 1 and ti + 1 < ntl:
                        load_x(ti + 1)
                    q = cnt["sg"] % 2
                    cnt["sg"] += 1
                    T.op("act", ACT(SG[q][:, :nt], PG[pb][:, :nt], AF.Silu), [("PG", pb)], [("SG", q)])
                    T.op("dve", TT_(G[:, s, :nt], SG[q][:, :nt], PU[pb][:, :nt], ALU.mult),
                         [("SG", q), ("PU", pb)], [("G", s)])
                if ti + 1 < ntl:
                    norm_in(ti + 1)
                for m in range(NC):
                    u = ti * NC + m
                    sl, pb = u % 2, u % 2
                    steps = [(MM(PY[pb][:, :nt], WO[sl][:, k, :], G[:, k, :nt], k == 0, k == NK - 1),
                              [("WO", sl), ("G", k)]) for k in range(NK)]
                    T.mmgroup(steps, [("PY", pb)])
                    load_wo(u + 2)
                    T.op("dve", STT(Xb[b][:, m, :nt], PY[pb][:, :nt], col(HG, m), Xb[b][:, m, :nt],
                                    ALU.mult, ALU.add), [("PY", pb), ("X", b, m), ("COLS", HG)], [("X", b, m)])
                    stat_step(1, b, m, nt, m == 0, m == NC - 1)
                    if mode == "ffn1" and kind == "x":
                        dst = self.x1T[:, off:off + nt].rearrange("(c p) n -> p c n", p=128)[:, m, :]
                        T.dma("sp", f"xs{b}", [(DMA(dst, Xb[b][:, m, :nt]), [("X", b, m)], [("x1T", idx)])],
                              wait_prev=(m == 0))
                rstd_from(1, nt)
                if mode == "ffn1":
                    A2, SH2 = ("A2x", "SH2x") if kind == "x" else ("A2c", "SH2c")
                    for c in range(NC):
                        g = cnt["stg"] % 4
                        cnt["stg"] += 1
                        modulate(b, c, nt, 1, A2, SH2, STG[g][:, :nt], ("STG", g))
                        T.dma("sp", f"st{g}", [(DMA(self.xn2[ti, :, c, :nt], STG[g][:, :nt]), [("STG", g)],
                                                 [("xn2", ti)])])
                else:
                    for c in range(NC):
                        g = cnt["stf"] % 2
                        cnt["stf"] += 1
                        T.op("dve", STT(STF[g][:, :nt], Xb[b][:, c, :nt], col("FN", c), RS[1][:, :nt],
                                        ALU.mult, ALU.mult), [("X", b, c), ("RS", 1), ("COLS", "FN")],
                             [("STF", g)])
                        dst = self.oT[:, off:off + nt].rearrange("(c p) n -> p c n", p=128)[:, c, :]
                        T.dma("sp", f"st{g}", [(DMA(dst, STF[g][:, :nt]), [("STF", g)], [("oT", idx)])])
            T.barrier()
            self.run_block()


def build_program(cfg):
    return Builder(cfg).build()


def _cols(v, nchunk):
    return np.ascontiguousarray(np.asarray(v, np.float32).reshape(nchunk, 128).T)


def prep_shared(cfg, inp):
    D, NC, NK = cfg.D, cfg.NC, cfg.NK
    f = lambda a: np.asarray(a, np.float32)
    sh = {}
    wa = f(inp["w_ada"])[0]
    sh["w_ada_h"] = np.ascontiguousarray(wa.reshape(NC, 128, cfg.NADA, 512).transpose(2, 1, 0, 3))
    sh["b_ada_h"] = _cols(f(inp["b_ada"])[0], 9 * NC)
    sh["nrm_h"] = np.ascontiguousarray(np.stack([
        _cols(f(inp["norm_ffn1"])[0], NC), _cols(f(inp["norm_mix"])[0], NC),
        _cols(f(inp["norm_ffn2"])[0], NC), _cols(f(inp["final_norm"]), NC)], axis=1))

    def ffn_in(w):
        return np.ascontiguousarray(w.reshape(NC, 128, 2, NK, 128).transpose(3, 1, 0, 2, 4))

    def ffn_out(w):
        return np.ascontiguousarray(w.reshape(NK, 128, NC, 128).transpose(2, 1, 0, 3))

    sh["w1in_h"] = ffn_in(f(inp["w_ffn1_in"])[0])
    sh["w1out_h"] = ffn_out(f(inp["w_ffn1_out"])[0])
    if cfg.stages >= 2:
        win = f(inp["w_in"])[0]

        def colslab(c0, n):
            return win[:, c0:c0 + n].reshape(NC, 128, n).transpose(1, 0, 2)

        fm_starts = [0, 256, 512, 768] + [2048 + 256 * i for i in range(4)] + [3088 + 256 * i for i in range(4)]
        sh["win_fm_h"] = np.ascontiguousarray(np.stack([colslab(c0, 256) for c0 in fm_starts]))
        tm_starts = [1024, 1536, 4112, 4624]
        sh["win_tm_h"] = np.ascontiguousarray(np.stack([colslab(c0, 512) for c0 in tm_starts]))
        sh["wg_h"] = np.ascontiguousarray(colslab(3072, 16)[None])
        cw = f(inp["conv_w"])[0]
        cb = f(inp["conv_b"])[0]
        cwb = np.concatenate([cw, cb[None]], axis=0)
        sh["convw_h"] = np.ascontiguousarray(cwb.reshape(6, 8, 128).transpose(2, 1, 0))
        bi, bf_ = f(inp["b_igate"])[0], f(inp["b_fgate"])[0]
        sh["gbias_h"] = np.ascontiguousarray(np.stack([bi, bf_], axis=1).reshape(16))
        sh["gnorm_h"] = np.ascontiguousarray(f(inp["gmlp_norm"])[0])
    if cfg.stages >= 5:
        sh["wsT_h"] = np.ascontiguousarray(f(inp["gmlp_w"])[0].transpose(2, 0, 1))
        sh["gmlpb_h"] = np.ascontiguousarray(f(inp["gmlp_b"])[0].reshape(1024))
        sh["mnorm_h"] = _cols(f(inp["mlstm_norm"])[0], 8)
        wo = f(inp["w_out"])[0]
        sh["wout_h"] = np.ascontiguousarray(wo.reshape(NC, 128, NC, 128).transpose(2, 1, 0, 3))
    if cfg.stages >= 6:
        sh["w2in_h"] = ffn_in(f(inp["w_ffn2_in"])[0])
        sh["w2out_h"] = ffn_out(f(inp["w_ffn2_out"])[0])
    return sh


def prep_core(cfg, inp, b):
    f = lambda a: np.asarray(a, np.float32)
    NC = cfg.NC
    d = {}
    d["xT"] = np.ascontiguousarray(f(inp["x"])[b].T)
    d["ctxT"] = np.ascontiguousarray(f(inp["ctx"])[b].T)
    d["cc"] = np.ascontiguousarray(np.stack([_cols(f(inp["c"])[b], NC), _cols(f(inp["c_ctx"]), NC)], axis=2))
    return d


def kernel(**inputs):
    cfg = Cfg()
    B = np.asarray(inputs["x"]).shape[0]
    assert B == 8
    nc = build_program(cfg)
    sh = prep_shared(cfg, inputs)
    in_maps = []
    for b in range(B):
        d = dict(sh)
        d.update(prep_core(cfg, inputs, b))
        in_maps.append(d)
    res = run_bass_kernel_spmd(nc, in_maps, core_ids=list(range(B)))
    out = np.stack([np.asarray(res.results[b]["oT"], np.float32).T for b in range(B)], axis=0)
    return np.ascontiguousarray(out)
```
